# Optimizing a Trainium2 kernel written in Bass

```python
import jax, jax.numpy as jnp
from jax import lax
import numpy as np

D_MODEL = 2048
BATCH = 16
SEQ = 256
DEPTH = 2
DEC_BATCH = 4
DEC_SEQ = 1024
PAST_LEN = 512

GRID_W = 64
N_MIXERS = 2
N_ATTN_LAYERS = (DEPTH + 1) // 2
N_LRU_LAYERS = DEPTH // 2
HEAD_DIM = 128
N_HEADS = D_MODEL // HEAD_DIM
N_KV_HEADS = N_HEADS // 4
Q_PER_KV = N_HEADS // N_KV_HEADS
ATTN_WIDTH = N_HEADS * HEAD_DIM
KV_WIDTH = N_KV_HEADS * HEAD_DIM
AXIS_DIM = HEAD_DIM // 2
ROPE_THETA = 10000.0
Q_BLOCK = 128
LRU_WIDTH = D_MODEL
LRU_BLOCKS = 8
LRU_BLOCK_DIM = LRU_WIDTH // LRU_BLOCKS
CONV_WIDTH = 4
CONV_LEFT = (CONV_WIDTH - 1) // 2
CONV_RIGHT = CONV_WIDTH - 1 - CONV_LEFT
RG_C = 8.0
EPS = 1e-6

kernel_name = 'hybrid_dit_gqa_rglru_step'

F32 = jnp.float32


def rms_norm(x, g):
    xf = x.astype(F32)
    y = xf * lax.rsqrt(jnp.mean(xf * xf, axis=-1, keepdims=True) + EPS)
    return (y * g.astype(F32)).astype(x.dtype)


def adaln(cond, w_mod, b_mod):
    m = jax.nn.silu(cond) @ w_mod + b_mod
    shift, scale, gate = jnp.split(m[:, None, :], 3, axis=-1)
    return shift, scale, gate


def axial_rope_tables(n):
    rows = n // GRID_W
    row = jnp.repeat(jnp.arange(rows, dtype=F32), GRID_W)
    col = jnp.tile(jnp.arange(GRID_W, dtype=F32), rows)
    inv = ROPE_THETA ** (-jnp.arange(0, AXIS_DIM, 2, dtype=F32) / AXIS_DIM)
    ar = row[:, None] * inv
    ac = col[:, None] * inv
    ang = jnp.concatenate([ar, ar, ac, ac], axis=-1)
    return jnp.cos(ang), jnp.sin(ang)


def rotate_half(u):
    h = u.shape[-1] // 2
    return jnp.concatenate([-u[..., h:], u[..., :h]], axis=-1)


def apply_axial_rope(x, cos, sin):
    rot = jnp.concatenate([rotate_half(x[..., :AXIS_DIM]), rotate_half(x[..., AXIS_DIM:])], axis=-1)
    out = x.astype(F32) * cos[None, :, None, :] + rot.astype(F32) * sin[None, :, None, :]
    return out.astype(x.dtype)


def blocked_attention(q, k, v):
    b, nq = q.shape[0], q.shape[1]
    nb = nq // Q_BLOCK
    qb = q.reshape(b, nb, Q_BLOCK, N_KV_HEADS, Q_PER_KV, HEAD_DIM).swapaxes(0, 1)
    scale = HEAD_DIM ** -0.5

    def one_block(qblk):
        s = jnp.einsum('bqkgd,bskd->bkgqs', qblk, k, preferred_element_type=F32) * scale
        p = jax.nn.softmax(s, axis=-1).astype(v.dtype)
        return jnp.einsum('bkgqs,bskd->bqkgd', p, v)

    o = lax.map(one_block, qb)
    return o.swapaxes(0, 1).reshape(b, nq, ATTN_WIDTH)


def attn_in(h, w_in, q_norm, k_norm):
    b, n, _ = h.shape
    p = h @ w_in
    q, k, v, g = jnp.split(p, [ATTN_WIDTH, ATTN_WIDTH + KV_WIDTH, ATTN_WIDTH + 2 * KV_WIDTH], axis=-1)
    q = rms_norm(q.reshape(b, n, N_HEADS, HEAD_DIM), q_norm)
    k = rms_norm(k.reshape(b, n, N_KV_HEADS, HEAD_DIM), k_norm)
    v = v.reshape(b, n, N_KV_HEADS, HEAD_DIM)
    return q, k, v, g


def attn_context(h, w_in, q_norm, k_norm, w_out):
    q, k, v, g = attn_in(h, w_in, q_norm, k_norm)
    o = blocked_attention(q, k, v)
    return (o * jax.nn.silu(g)) @ w_out, k, v


def attn_latent(h, ck, cv, w_in, q_norm, k_norm, w_out):
    q, k, v, g = attn_in(h, w_in, q_norm, k_norm)
    cos, sin = axial_rope_tables(h.shape[1])
    q = apply_axial_rope(q, cos, sin)
    k = apply_axial_rope(k, cos, sin)
    keys = jnp.concatenate([ck.astype(k.dtype), k], axis=1)
    vals = jnp.concatenate([cv.astype(v.dtype), v], axis=1)
    o = blocked_attention(q, keys, vals)
    return (o * jax.nn.silu(g)) @ w_out


def centred_depthwise_conv(x, w, b):
    n = x.shape[1]
    xp = jnp.pad(x, ((0, 0), (CONV_LEFT, CONV_RIGHT), (0, 0)))
    return sum((xp[:, j:j + n] * w[j] for j in range(CONV_WIDTH)), b)


def block_diag_linear(x, w, b):
    bsz, n, _ = x.shape
    y = jnp.einsum('bnhi,hij->bnhj', x.reshape(bsz, n, LRU_BLOCKS, LRU_BLOCK_DIM), w)
    return y.reshape(bsz, n, LRU_WIDTH) + b


def rglru_coeffs(x, w_a, b_a, w_x, b_x, lam):
    r = jax.nn.sigmoid(block_diag_linear(x, w_a, b_a).astype(F32))
    i = jax.nn.sigmoid(block_diag_linear(x, w_x, b_x).astype(F32))
    log_a = RG_C * r * jax.nn.log_sigmoid(lam.astype(F32))
    a = jnp.exp(log_a)
    mult = jnp.sqrt(-jnp.expm1(2.0 * log_a))
    return a, mult * i * x.astype(F32)


def _combine(e1, e2):
    a1, b1 = e1
    a2, b2 = e2
    return a1 * a2, a2 * b1 + b2


def linear_scan(a, b, h0, reverse):
    idx = -1 if reverse else 0
    b = b.at[:, idx].add(a[:, idx] * h0)
    _, h = lax.associative_scan(_combine, (a, b), axis=1, reverse=reverse)
    return h


def lru_mixer(h, h0_fwd, h0_bwd, w_in, conv_w, conv_b, w_a, b_a, w_x, b_x, lam, w_out):
    xb, gb = jnp.split(h @ w_in, 2, axis=-1)
    xb = centred_depthwise_conv(xb, conv_w, conv_b)
    hs = []
    finals = []
    for d, (h0, rev) in enumerate(((h0_fwd, False), (h0_bwd, True))):
        a, b = rglru_coeffs(xb, w_a[d], b_a[d], w_x[d], b_x[d], lam[d])
        hd = linear_scan(a, b, h0.astype(F32), rev)
        hs.append(hd)
        finals.append(hd[:, 0] if rev else hd[:, -1])
    y = (hs[0] + hs[1]).astype(h.dtype)
    out = (y * jax.nn.silu(gb)) @ w_out
    return out, jnp.stack(finals, axis=1).astype(h.dtype)


def setup_inputs(seed: int = 0) -> dict:
    key = jax.random.key(seed)
    ks = jax.random.split(key, 24)

    def nrm(k, shape, std):
        return std * jax.random.normal(k, shape, F32)

    a0 = jax.random.uniform(ks[22], (N_LRU_LAYERS, 2, LRU_WIDTH), F32, minval=0.9, maxval=0.999)
    s = a0 ** (1.0 / RG_C)
    lru_lambda = jnp.log(s) - jnp.log1p(-s)
    in_w = ATTN_WIDTH + 2 * KV_WIDTH + ATTN_WIDTH
    return {
        'x_prompt': nrm(ks[0], (BATCH, SEQ, D_MODEL), 1.0),
        'x_sample': nrm(ks[1], (DEC_BATCH, DEC_SEQ, D_MODEL), 1.0),
        'c': nrm(ks[2], (DEC_BATCH, D_MODEL), 1.0),
        'cache_k': nrm(ks[3], (DEC_BATCH, N_ATTN_LAYERS, PAST_LEN, N_KV_HEADS, HEAD_DIM), 1.0),
        'cache_v': nrm(ks[4], (DEC_BATCH, N_ATTN_LAYERS, PAST_LEN, N_KV_HEADS, HEAD_DIM), 1.0),
        'state_lru': nrm(ks[5], (DEC_BATCH, N_LRU_LAYERS, 2, LRU_WIDTH), 0.5),
        'c_ctx': nrm(ks[6], (D_MODEL,), 1.0),
        'w_mod': nrm(ks[7], (DEPTH, D_MODEL, 3 * D_MODEL), 0.5 * D_MODEL ** -0.5),
        'b_mod': nrm(ks[8], (DEPTH, 3 * D_MODEL), 0.01),
        'g_pre': 1.0 + nrm(ks[9], (DEPTH, D_MODEL), 0.05),
        'g_post': 1.0 + nrm(ks[10], (DEPTH, D_MODEL), 0.05),
        'w_in_attn': nrm(ks[11], (N_ATTN_LAYERS, D_MODEL, in_w), D_MODEL ** -0.5),
        'q_norm': 1.0 + nrm(ks[12], (N_ATTN_LAYERS, HEAD_DIM), 0.05),
        'k_norm': 1.0 + nrm(ks[13], (N_ATTN_LAYERS, HEAD_DIM), 0.05),
        'w_out_attn': nrm(ks[14], (N_ATTN_LAYERS, ATTN_WIDTH, D_MODEL), ATTN_WIDTH ** -0.5),
        'w_in_lru': nrm(ks[15], (N_LRU_LAYERS, D_MODEL, 2 * LRU_WIDTH), D_MODEL ** -0.5),
        'conv_w': nrm(ks[16], (N_LRU_LAYERS, CONV_WIDTH, LRU_WIDTH), CONV_WIDTH ** -0.5),
        'conv_b': nrm(ks[17], (N_LRU_LAYERS, LRU_WIDTH), 0.01),
        'w_rg_a': nrm(ks[18], (N_LRU_LAYERS, 2, LRU_BLOCKS, LRU_BLOCK_DIM, LRU_BLOCK_DIM), LRU_BLOCK_DIM ** -0.5),
        'b_rg_a': nrm(ks[19], (N_LRU_LAYERS, 2, LRU_WIDTH), 0.01),
        'w_rg_x': nrm(ks[20], (N_LRU_LAYERS, 2, LRU_BLOCKS, LRU_BLOCK_DIM, LRU_BLOCK_DIM), LRU_BLOCK_DIM ** -0.5),
        'b_rg_x': nrm(ks[21], (N_LRU_LAYERS, 2, LRU_WIDTH), 0.01),
        'lru_lambda': lru_lambda,
        'w_out_lru': nrm(ks[23], (N_LRU_LAYERS, LRU_WIDTH, D_MODEL), LRU_WIDTH ** -0.5),
    }


def reference(x_prompt, x_sample, c, cache_k, cache_v, state_lru, c_ctx, w_mod, b_mod, g_pre, g_post,
              w_in_attn, q_norm, k_norm, w_out_attn, w_in_lru, conv_w, conv_b,
              w_rg_a, b_rg_a, w_rg_x, b_rg_x, lru_lambda, w_out_lru):
    y_p = x_prompt
    y_s = x_sample
    new_k, new_v, new_h = [], [], []
    for l in range(DEPTH):
        j = l // N_MIXERS
        sh_c, sc_c, ga_c = adaln(c_ctx[None, :], w_mod[l], b_mod[l])
        sh_s, sc_s, ga_s = adaln(c, w_mod[l], b_mod[l])
        hp = rms_norm(y_p, g_pre[l]) * (1.0 + sc_c) + sh_c
        hs = rms_norm(y_s, g_pre[l]) * (1.0 + sc_s) + sh_s
        if l % N_MIXERS == 0:
            mp, kc, vc = attn_context(hp, w_in_attn[j], q_norm[j], k_norm[j], w_out_attn[j])
            ms = attn_latent(hs, cache_k[:, j], cache_v[:, j], w_in_attn[j], q_norm[j], k_norm[j], w_out_attn[j])
            new_k.append(kc)
            new_v.append(vc)
        else:
            zeros = jnp.zeros((hp.shape[0], LRU_WIDTH), F32)
            mp, fin = lru_mixer(hp, zeros, zeros, w_in_lru[j], conv_w[j], conv_b[j],
                                w_rg_a[j], b_rg_a[j], w_rg_x[j], b_rg_x[j], lru_lambda[j], w_out_lru[j])
            ms, _ = lru_mixer(hs, state_lru[:, j, 0], state_lru[:, j, 1], w_in_lru[j], conv_w[j], conv_b[j],
                              w_rg_a[j], b_rg_a[j], w_rg_x[j], b_rg_x[j], lru_lambda[j], w_out_lru[j])
            new_h.append(fin)
        y_p = y_p + ga_c * rms_norm(mp, g_post[l])
        y_s = y_s + ga_s * rms_norm(ms, g_post[l])
    new_cache_k = jnp.stack(new_k, axis=1)
    new_cache_v = jnp.stack(new_v, axis=1)
    new_state_lru = jnp.stack(new_h, axis=1)
    return (y_p, y_s, new_cache_k, new_cache_v, new_state_lru)
```

```python
import contextlib
import numpy as np
import concourse.bass as bass
import concourse.mybir as mybir
from concourse.bass_utils import run_bass_kernel_spmd

F32 = mybir.dt.float32
BF16 = mybir.dt.bfloat16
AF = mybir.ActivationFunctionType
ALU = mybir.AluOpType

NT = 1024
D = 2048
KC = 16
EPS = 1e-6
NEG = -30000.0

PP = {}
_o = 0
for _n, _w in [("bmod0", 48), ("bmod1", 48), ("gpre0", 16), ("gpre1", 16), ("gpost0", 16), ("gpost1", 16),
               ("qn", 1), ("kn", 1), ("convw", 64), ("convb", 16), ("ba", 32), ("bx", 32), ("lam", 32)]:
    PP[_n] = (_o, _w)
    _o += _w
NPP = _o


class Res:
    __slots__ = ("name", "w", "rd")

    def __init__(self, name="", deps=None):
        self.name = name
        self.w = None
        self.rd = list(deps) if deps else []


def retire(rs):
    out = []
    for r in rs:
        if r.w is not None:
            out.append(r.w)
        out.extend(r.rd)
    return list(dict.fromkeys(out))


class Op:
    __slots__ = ("fn", "deps", "signal", "dma", "dma_sem", "dma_val", "pre")

    def __init__(self, fn, deps, signal):
        self.fn = fn
        self.deps = deps
        self.signal = signal
        self.dma = False
        self.dma_sem = None
        self.dma_val = 0
        self.pre = None


class Plan:
    ENGS = ("pe", "act", "dve", "pool", "sp")
    NDMASEM = {"sp": 12, "pool": 12}

    def __init__(self):
        self.ops = {e: [] for e in self.ENGS}
        self.dma_count = {q: 0 for q in self.NDMASEM}

    def _deps(self, eng, reads, writes):
        deps = []
        for r in reads:
            if r.w is not None:
                deps.append(r.w)
        for w in writes:
            if w.w is not None:
                deps.append(w.w)
            deps.extend(w.rd)
        out = []
        for d in deps:
            if d[0] == "eng" and d[1] == eng and eng == "pe":
                continue
            out.append(d)
        return out

    def op(self, eng, fn, reads=(), writes=(), signal=True):
        deps = self._deps(eng, reads, writes)
        idx = len(self.ops[eng])
        self.ops[eng].append(Op(fn, deps, signal))
        tag = ("eng", eng, idx)
        for r in reads:
            r.rd.append(tag)
        for w in writes:
            w.w = tag
            w.rd = []
        return tag

    def dma(self, q, fn, reads=(), writes=()):
        deps = self._deps(q, reads, writes)
        n = self.dma_count[q]
        self.dma_count[q] = n + 1
        K = self.NDMASEM[q]
        semkey = (q, n % K)
        val = 16 * (n // K + 1)
        o = Op(fn, deps, False)
        o.dma = True
        o.dma_sem = semkey
        o.dma_val = val
        if n >= K:
            o.pre = (semkey, val - 16)
        self.ops[q].append(o)
        tag = ("dma", semkey, val)
        for r in reads:
            r.rd.append(tag)
        for w in writes:
            w.w = tag
            w.rd = []
        return tag

    def emit(self, nc, final_waits=()):
        with contextlib.ExitStack() as st:
            esem = {e: st.enter_context(nc.semaphore("s_" + e)) for e in self.ENGS}
            dsem = {}
            for q, K in self.NDMASEM.items():
                for i in range(K):
                    dsem[(q, i)] = st.enter_context(nc.semaphore("d_%s_%d" % (q, i)))
            sigcount = {}
            for e in self.ENGS:
                ops = self.ops[e]
                cnt = 0
                after = [0] * len(ops)
                for i, o in enumerate(ops):
                    if o.signal and not o.dma:
                        cnt += 1
                    after[i] = cnt
                need = [None] * len(ops)
                nxt = None
                for i in range(len(ops) - 1, -1, -1):
                    if ops[i].signal and not ops[i].dma:
                        nxt = after[i]
                    need[i] = nxt
                sigcount[e] = need
            final = []
            for r in final_waits:
                if r.w is not None:
                    final.append(r.w)

            def resolve(d):
                if d[0] == "eng":
                    v = sigcount[d[1]][d[2]]
                    assert v is not None, "dependency on unsignalled tail op %s" % (d,)
                    return (("e", d[1]), v)
                return (("d", d[1]), d[2])

            def semof(k):
                return esem[k[1]] if k[0] == "e" else dsem[k[1]]

            def run(e, engine):
                waited = {}
                for o in self.ops[e]:
                    ws = [resolve(d) for d in o.deps]
                    if o.pre is not None:
                        ws.append((("d", o.pre[0]), o.pre[1]))
                    for k, v in ws:
                        if waited.get(k, 0) >= v:
                            continue
                        waited[k] = v
                        engine.wait_ge(semof(k), v)
                    ins = o.fn(engine)
                    if o.dma:
                        ins.then_inc(dsem[o.dma_sem], 16)
                    elif o.signal:
                        ins.then_inc(esem[e], 1)
                if e == "sp":
                    for d in final:
                        k, v = resolve(d)
                        if waited.get(k, 0) >= v:
                            continue
                        waited[k] = v
                        engine.wait_ge(semof(k), v)

            with nc.Block() as block:
                @block.tensor
                def _(eng):
                    run("pe", eng)

                @block.scalar
                def _(eng):
                    run("act", eng)

                @block.vector
                def _(eng):
                    run("dve", eng)

                @block.gpsimd
                def _(eng):
                    run("pool", eng)

                @block.sync
                def _(eng):
                    run("sp", eng)


class _Stop(Exception):
    pass


def build_program(depth=2, stop=None):
    nc = bass.Bass("TRN2", target_bir_lowering=False)

    def din(name, shape):
        return nc.dram_tensor(name, list(shape), F32, kind="ExternalInput").ap()

    def dout(name, shape):
        return nc.dram_tensor(name, list(shape), F32, kind="ExternalOutput").ap()

    x_d = din("x", [NT, D])
    cond_d = din("cond", [128, 16])
    ck_d = din("ck", [512, 512])
    cv_d = din("cv", [512, 512])
    h0_d = din("h0", [128, 32])
    flags_d = din("flags", [128, 8])
    maskb_d = din("maskb", [128, 48])
    cos_d = din("cosT", [128, NT])
    sin_d = din("sinT", [128, NT])
    ident_d = din("ident", [128, 128])
    rperm_d = din("rperm", [128, 128])
    pp_d = din("pp", [128, NPP])
    wmod_d = din("w_mod", [2, D, 3 * D])
    wia_d = din("w_in_attn", [D, 5120])
    woa_d = din("w_out_attn", [D, D])
    wil_d = din("w_in_lru", [D, 4096])
    wol_d = din("w_out_lru", [D, D])
    wra_d = din("w_rg_a", [2, 8, 256, 256])
    wrx_d = din("w_rg_x", [2, 8, 256, 256])

    y_o = dout("y_out", [NT, D])
    k_o = dout("k_out", [NT, 512])
    v_o = dout("v_out", [NT, 512])
    st_o = dout("st_out", [128, 128])

    P = Plan()
    finals = []

    with contextlib.ExitStack() as st:
        try:
            def chk(ph):
                if stop == ph:
                    raise _Stop()

            def sb(name, shape, dt_=F32):
                return st.enter_context(nc.sbuf_tensor("sb_" + name, list(shape), dt_))

            A1 = sb("A1", [128, 16384])
            A2 = sb("A2", [128, 8192])
            A3 = sb("A3", [128, 8192])
            A4 = sb("A4", [128, 8192])
            NRING = 4
            RING = [sb("ring%d" % i, [128, 4096], BF16) for i in range(NRING)]
            ring_res = [Res("ring%d" % i) for i in range(NRING)]
            ring_n = [0]
            banks = [st.enter_context(nc.psum_tensor("bank%d" % i, [128, 512], F32)) for i in range(8)]
            bres = [Res("bank%d" % i) for i in range(8)]

            identf = sb("identf", [128, 128])
            identb = sb("identb", [128, 128], BF16)
            rpermb = sb("rpermb", [128, 128], BF16)
            onesb = sb("onesb", [128, 128], BF16)
            one11 = sb("one11", [1, 2])
            pp = sb("pp", [128, NPP])
            cond = sb("cond", [128, 16])
            condt = sb("condt", [128, 16])
            scond = sb("scond", [128, 16], BF16)
            scondf = sb("scondf", [128, 16])
            onesf = sb("onesf", [128, 2])
            h0t = sb("h0t", [128, 32])
            flags = sb("flags", [128, 8])
            maskb = sb("maskb", [128, 48])
            modT = [sb("modT%d" % l, [128, 48]) for l in range(2)]
            gmod = [sb("gmod%d" % l, [128, 16]) for l in range(2)]
            ggv = [sb("ggv%d" % l, [128, 16]) for l in range(2)]
            gq = sb("gq", [128, 2])
            wneg = sb("wneg", [128, 64])
            cl = sb("cl", [128, 32])
            hcl = sb("hcl", [128, 32])
            hba = sb("hba", [128, 32])
            hbx = sb("hbx", [128, 32])
            fin = sb("fin", [128, 128])
            fino = sb("fino", [128, 128])

            r_const = Res("const")
            r_pp = Res("pp")
            r_small = Res("small")
            r_scond = Res("scond")
            r_mod = [Res("mod0"), Res("mod1")]
            r_modsc = [Res("modsc0"), Res("modsc1")]
            r_modg = [Res("modg0"), Res("modg1")]
            r_der = Res("derived")
            r_fin = Res("fin")

            def ppc(name, j=0, w=1):
                o, _ = PP[name]
                return pp[:, o + j:o + j + w]

            def mm(out, lhsT, rhs, start, stop, reads, writes, signal=False):
                P.op("pe", lambda e: e.matmul(out, lhsT=lhsT, rhs=rhs, start=start, stop=stop),
                     reads=reads, writes=writes, signal=signal)

            def tr(out, in_, ident, reads, writes, signal=False):
                P.op("pe", lambda e: e.transpose(out=out, in_=in_, identity=ident), reads=reads, writes=writes,
                     signal=signal)

            def act(out, in_, func, reads, writes, bias=None, scale=None):
                kw = {}
                if bias is not None:
                    kw["bias"] = bias
                if scale is not None:
                    kw["scale"] = scale
                P.op("act", lambda e: e.activation(out=out, in_=in_, func=func, **kw), reads=reads, writes=writes)

            def tt(eng, out, in0, in1, op, reads, writes):
                P.op(eng, lambda e: e.tensor_tensor(out=out, in0=in0, in1=in1, op=op), reads=reads, writes=writes)

            def ts(eng, out, in0, s1, s2, op0, op1, reads, writes):
                if s2 is None:
                    P.op(eng, lambda e: e.tensor_scalar(out=out, in0=in0, scalar1=s1, scalar2=None, op0=op0),
                         reads=reads, writes=writes)
                else:
                    P.op(eng, lambda e: e.tensor_scalar(out=out, in0=in0, scalar1=s1, scalar2=s2, op0=op0, op1=op1),
                         reads=reads, writes=writes)

            def stt(out, in0, scalar, in1, op0, op1, reads, writes):
                P.op("dve", lambda e: e.scalar_tensor_tensor(out=out, in0=in0, scalar=scalar, in1=in1, op0=op0, op1=op1),
                     reads=reads, writes=writes)

            def cp(eng, out, in_, reads, writes):
                if eng == "act":
                    act(out, in_, AF.Copy, reads, writes)
                else:
                    P.op(eng, lambda e: e.tensor_copy(out=out, in_=in_), reads=reads, writes=writes)

            def recip(out, in_, reads, writes):
                P.op("dve", lambda e: e.reciprocal(out=out, in_=in_), reads=reads, writes=writes)

            def wload(dst_view_fn, src):
                s = ring_n[0] % NRING
                ring_n[0] += 1
                P.dma("pool", lambda e: e.dma_start(out=dst_view_fn(RING[s]), in_=src), writes=[ring_res[s]])
                return s

            def v3(ap2, c):
                return ap2.rearrange("p (c t) -> p c t", c=c)

            P.dma("sp", lambda e: e.dma_start(out=identf[:], in_=ident_d), writes=[r_const])
            P.dma("sp", lambda e: e.dma_start(out=pp[:], in_=pp_d), writes=[r_pp])
            r_cond, r_h0, r_flags, r_maskb = Res(), Res(), Res(), Res()
            P.dma("sp", lambda e: e.dma_start(out=cond[:], in_=cond_d), writes=[r_cond])
            P.dma("sp", lambda e: e.dma_start(out=h0t[:], in_=h0_d), writes=[r_h0])
            P.dma("sp", lambda e: e.dma_start(out=flags[:], in_=flags_d), writes=[r_flags])
            P.dma("sp", lambda e: e.dma_start(out=maskb[:], in_=maskb_d), writes=[r_maskb])
            r_cb = Res("constb")
            r_rp = Res("rpermb")
            P.dma("pool", lambda e: e.dma_start(out=identb[:], in_=ident_d), writes=[r_cb])
            P.dma("pool", lambda e: e.dma_start(out=rpermb[:], in_=rperm_d), writes=[r_rp])
            r_ones = Res("ones")
            P.op("dve", lambda e: e.memset(onesb[:], 1.0), writes=[r_ones])
            P.op("dve", lambda e: e.memset(one11[:], 1.0), writes=[r_ones])
            act(condt[:], cond[:], AF.Tanh, [r_cond], [r_scond], scale=0.5)
            stt(condt[:], condt[:], 1.0, cond[:], ALU.add, ALU.mult, [r_scond, r_cond], [r_scond])
            ts("dve", scond[:], condt[:], 0.5, None, ALU.mult, None, [r_scond], [r_scond])
            ts("dve", scondf[:], condt[:], 0.5, None, ALU.mult, None, [r_scond], [r_scond])
            P.op("dve", lambda e: e.memset(onesf[:], 1.0), writes=[r_ones])
            ts("dve", gq[:, 0:1], ppc("qn"), float(128.0 ** -0.5), None, ALU.mult, None, [r_pp], [r_der])
            cp("dve", gq[:, 1:2], ppc("kn"), [r_pp], [r_der])
            ts("dve", wneg[:], ppc("convw", 0, 64), flags[:, 1:2], None, ALU.mult, None, [r_pp, r_flags], [r_der])
            ts("dve", hba[:], ppc("ba", 0, 32), 0.5, None, ALU.mult, None, [r_pp], [r_der])
            ts("dve", hbx[:], ppc("bx", 0, 32), 0.5, None, ALU.mult, None, [r_pp], [r_der])
            act(cl[:], ppc("lam", 0, 32), AF.Exp, [r_pp], [r_der], scale=-1.0)
            act(cl[:], cl[:], AF.Ln, [r_der], [r_der], bias=1.0)
            ts("dve", hcl[:], cl[:], -4.0, None, ALU.mult, None, [r_der], [r_der])
            ts("dve", cl[:], cl[:], -8.0, None, ALU.mult, None, [r_der], [r_der])

            bg_acc = {"t": [A4[:, 6144 + i * 512:6144 + (i + 1) * 512] for i in range(2)],
                      "r": [Res(), Res()]}
            bg_pending = []

            def bg_load(l, grp):
                slots = []
                for kh in range(2):
                    src = wmod_d[l, kh * 1024:(kh + 1) * 1024, grp * 512:(grp + 1) * 512].rearrange(
                        "(k p) n -> p k n", p=128)
                    slots.append(wload(lambda r: v3(r[:], 8), src))
                return slots

            def adaln_group_dve(l, grp, tbank, acc=None, acc_res=None, slots=None, defer=None):
                if acc is None:
                    ai = grp % 2
                    acc, acc_res = bg_acc["t"][ai], bg_acc["r"][ai]
                for kh in range(2):
                    if slots is None:
                        src = wmod_d[l, kh * 1024:(kh + 1) * 1024, grp * 512:(grp + 1) * 512].rearrange(
                            "(k p) n -> p k n", p=128)
                        s_ = wload(lambda r: v3(r[:], 8), src)
                    else:
                        s_ = slots[kh]
                    sv_ = v3(RING[s_][:], 8)
                    for k in range(8):
                        kk = kh * 8 + k
                        if kk == 0:
                            ts("dve", acc, sv_[:, k, :], scondf[:, 0:1], None, ALU.mult, None,
                               [ring_res[s_], r_scond], [acc_res])
                        else:
                            stt(acc, sv_[:, k, :], scondf[:, kk:kk + 1], acc, ALU.mult, ALU.add,
                                [ring_res[s_], r_scond, acc_res], [acc_res])
                        yield

                def finalize():
                    for j in range(4):
                        mm(banks[tbank][:, j:j + 1], acc[:, j * 128:(j + 1) * 128], onesf[:, 0:1], True, True,
                           [acc_res, r_ones], [bres[tbank]], signal=(j == 3))
                    o = PP["bmod%d" % l][0]
                    tt("dve", modT[l][:, 4 * grp:4 * grp + 4], banks[tbank][:, 0:4],
                       pp[:, o + 4 * grp:o + 4 * grp + 4], ALU.add, [bres[tbank], r_pp], [r_mod[l]])
                    if (l, grp) == (0, 11):
                        adaln_finish_g(0)
                    if (l, grp) == (1, 7):
                        adaln_finish_sc(1)
                    if (l, grp) == (1, 11):
                        adaln_finish_g(1)
                (defer if defer is not None else bg_pending).append(finalize)
                yield

            def adaln_finish_sc(l):
                o = PP["gpre%d" % l][0]
                stt(gmod[l][:], modT[l][:, 16:32], 1.0, pp[:, o:o + 16], ALU.add, ALU.mult, [r_mod[l], r_pp], [r_modsc[l]])

            def adaln_finish_g(l):
                o = PP["gpost%d" % l][0]
                tt("dve", ggv[l][:], modT[l][:, 32:48], pp[:, o:o + 16], ALU.mult, [r_mod[l], r_pp], [r_modg[l]])

            bg_jobs = [(0, g) for g in range(8, 12)] + ([(1, g) for g in range(12)] if depth > 1 else [])

            bg_cur = [None, 0]

            def run_bg_half(tbank):
                while bg_pending:
                    bg_pending.pop(0)()
                if bg_cur[0] is None:
                    if not bg_jobs:
                        return
                    l_, g_ = bg_jobs.pop(0)
                    bg_cur[0] = adaln_group_dve(l_, g_, tbank)
                    bg_cur[1] = 0
                for _ in range(8):
                    next(bg_cur[0])
                bg_cur[1] += 1
                if bg_cur[1] == 2:
                    for _ in bg_cur[0]:
                        pass
                    bg_cur[0] = None

            def run_bg(rowbank, tbank):
                while bg_pending:
                    bg_pending.pop(0)()
                if not bg_jobs:
                    return
                l_, g_ = bg_jobs.pop(0)
                for _ in adaln_group_dve(l_, g_, tbank):
                    pass

            chk(0)
            def load_xT(stage, stage_res, consume):
                for tb in range(8):
                    si = tb % 2
                    P.dma("sp", lambda e, tb=tb, si=si: e.dma_start(out=stage[si], in_=x_d[tb * 128:(tb + 1) * 128, :]),
                          writes=[stage_res[si]])
                    for q in range(4):
                        for j in range(4):
                            c = 4 * q + j
                            tr(banks[q][:, j * 128:(j + 1) * 128], stage[si][:, c * 128:(c + 1) * 128], identf[:],
                               [stage_res[si], r_const], [bres[q]], signal=(j == 3))
                        consume(tb, q)

            def prenorm(yT, y_res, l, hT, h_res, sqt, sq_res, rstd, rstd_res, tmpt, tmp_res):
                for half in range(2):
                    hs = slice(half * 512, (half + 1) * 512)
                    sbk = 4 + half
                    for c in range(16):
                        qi = c % 3
                        act(sqt[qi], yT[:, c, hs], AF.Square, [y_res[c][half]], [sq_res[qi]])
                        mm(banks[sbk][:], onesb[:], sqt[qi], c == 0, c == 15, [sq_res[qi], r_ones], [bres[sbk]],
                           signal=True)
                    act(rstd[half], banks[sbk][:], AF.Ln, [bres[sbk]], [rstd_res[half]], bias=EPS, scale=1.0 / D)
                    act(rstd[half], rstd[half], AF.Exp, [rstd_res[half]], [rstd_res[half]], scale=-0.5)
                    for c in range(16):
                        ti = c % 2
                        tt("dve", tmpt[ti], yT[:, c, hs], rstd[half], ALU.mult, [y_res[c][half], rstd_res[half]],
                           [tmp_res[ti]])
                        act(hT[:, c, hs], tmpt[ti], AF.Identity, [tmp_res[ti], r_modsc[l], r_mod[l]], [h_res[c][half]],
                            bias=modT[l][:, c:c + 1], scale=gmod[l][:, c:c + 1])

            def oproj_load(w_d, sidx):
                src = w_d[:, sidx * 256:(sidx + 1) * 256].rearrange("(k p) n -> p k n", p=128)
                return wload(lambda r: v3(r[:], 16), src)

            def outproj(w_d, ogT, og_res, mT, m_res, sqt, sq_res, rstd, rstd_res, before_slot=None, pre_slots=None):
                for sidx in range(8):
                    if pre_slots and sidx in pre_slots:
                        s = pre_slots[sidx]
                    else:
                        s = oproj_load(w_d, sidx)
                    if before_slot is not None:
                        before_slot(sidx)
                    sv = v3(RING[s][:], 16)
                    for mcl in range(2):
                        mc = 2 * sidx + mcl
                        for half in range(2):
                            hs = slice(half * 512, (half + 1) * 512)
                            bk = (2 * mc + half) % 4
                            for k in range(16):
                                mm(banks[bk][:], sv[:, k, mcl * 128:(mcl + 1) * 128], ogT[:, k, hs], k == 0, k == 15,
                                   [ring_res[s], og_res[k][half]], [bres[bk]], signal=(k == 15))
                            qi = (2 * mc + half) % 3
                            act(sqt[qi], banks[bk][:], AF.Square, [bres[bk]], [sq_res[qi], bres[bk]])
                            cp("dve", mT[:, mc, hs], banks[bk][:], [bres[bk]], [m_res[mc][half]])
                            mm(banks[4 + half][:], onesb[:], sqt[qi], mc == 0, mc == 15, [sq_res[qi], r_ones],
                               [bres[4 + half]], signal=True)
                for half in range(2):
                    act(rstd[half], banks[4 + half][:], AF.Ln, [bres[4 + half]], [rstd_res[half]], bias=EPS,
                        scale=1.0 / D)
                    act(rstd[half], rstd[half], AF.Exp, [rstd_res[half]], [rstd_res[half]], scale=-0.5)

            xT = v3(A1[:], 16)
            xT_res = [[Res() for _ in range(2)] for _ in range(16)]
            xs = [A3[:, 0:2048], A3[:, 2048:4096]]
            xs_res = [Res(), Res()]
            def consume_x(tb, q):
                half = tb // 4
                dst = xT[:, 4 * q:4 * q + 4, tb * 128:(tb + 1) * 128]
                src = v3(banks[q][:], 4)
                eng = "act" if (q % 2 == 0) else "dve"
                cp(eng, dst, src, [bres[q]], [xT_res[4 * q + j][half] for j in range(4)])
            load_xT(xs, xs_res, consume_x)
            for grp in range(8):
                while bg_pending:
                    bg_pending.pop(0)()
                for _ in adaln_group_dve(0, grp, 6):
                    pass
            while bg_pending:
                bg_pending.pop(0)()
            adaln_finish_sc(0)

            hT = v3(A2[:].bitcast(BF16), 16)
            hT_res = [[Res() for _ in range(2)] for _ in range(16)]
            a4b = A4[:].bitcast(BF16)
            sqt = [a4b[:, i * 512:(i + 1) * 512] for i in range(3)]
            sq_res = [Res() for _ in range(3)]
            rstd = [A4[:, 1024 + i * 512:1024 + (i + 1) * 512] for i in range(2)]
            rstd_res = [Res(), Res()]
            tmpt = [A4[:, 2048 + i * 512:2048 + (i + 1) * 512] for i in range(2)]
            tmp_res = [Res(), Res()]
            prenorm(xT, xT_res, 0, hT, hT_res, sqt, sq_res, rstd, rstd_res, tmpt, tmp_res)

            chk(1)
            dead = retire([r for row in xT_res for r in row])
            gT = v3(A1[:, 0:8192].bitcast(BF16), 16)
            gT_res = [[Res(deps=dead) for _ in range(2)] for _ in range(16)]
            cosT = A1[:, 8192:9216]
            sinT = A1[:, 9216:10240]
            r_rope = Res("rope", deps=dead)
            r_rope2 = Res("rope2", deps=dead)
            P.dma("sp", lambda e: e.dma_start(out=cosT, in_=cos_d), writes=[r_rope])
            P.dma("sp", lambda e: e.dma_start(out=sinT, in_=sin_d), writes=[r_rope2])
            a1b = A1[:, 10240:16384]
            a1bb = a1b.bitcast(BF16)
            sq2 = [a1bb[:, i * 512:(i + 1) * 512] for i in range(3)]
            sq2_res = [Res(deps=dead) for _ in range(3)]
            qn2 = [a1bb[:, 1536 + i * 512:1536 + (i + 1) * 512] for i in range(3)]
            qn2_res = [Res(deps=dead) for _ in range(3)]
            rs2 = [a1b[:, 1536 + i * 512:1536 + (i + 1) * 512] for i in range(2)]
            rs2_res = [Res(deps=dead) for _ in range(2)]
            t1 = [a1b[:, 2560 + i * 512:2560 + (i + 1) * 512] for i in range(2)]
            t1_res = [Res(deps=dead) for _ in range(2)]
            t2 = [a1b[:, 3584 + i * 512:3584 + (i + 1) * 512] for i in range(2)]
            t2_res = [Res(deps=dead) for _ in range(2)]
            tg = [a1b[:, 4608 + i * 512:4608 + (i + 1) * 512] for i in range(2)]
            tg_res = [Res(deps=dead) for _ in range(2)]
            vst = [a1b[:, 5632 + i * 256:5632 + (i + 1) * 256] for i in range(2)]
            vst_res = [Res(deps=dead) for _ in range(2)]

            dead4 = retire(sq_res + rstd_res + tmp_res)
            kT = v3(A4[:, 0:3072].bitcast(BF16), 4)
            kT_res = [[Res(deps=dead4) for _ in range(12)] for _ in range(4)]
            Vb = v3(A4[:, 3072:6144].bitcast(BF16), 12)
            V_res = [Res(deps=dead4) for _ in range(12)]
            ckst = v3(A4[:, 6144:8192], 4)
            ckst_res = Res(deps=dead4 + retire(bg_acc["r"]))
            dead3 = retire(xs_res)
            qT = v3(A3[:].bitcast(BF16), 16)
            qT_res = [[Res(deps=dead3) for _ in range(2)] for _ in range(16)]

            def inproj_slot(col0):
                src = wia_d[:, col0:col0 + 256].rearrange("(k p) n -> p k n", p=128)
                s = wload(lambda r: v3(r[:], 16), src)
                return s, v3(RING[s][:], 16)

            acc_n = [0]

            def acc_fm(sv, s, mcl, half, h_res_):
                bk = acc_n[0] % 4
                acc_n[0] += 1
                hs = slice(half * 512, (half + 1) * 512)
                for k in range(16):
                    mm(banks[bk][:], sv[:, k, mcl * 128:(mcl + 1) * 128], hT[:, k, hs], k == 0, k == 15,
                       [ring_res[s], h_res_[k][half]], [bres[bk]], signal=(k == 15))
                return bk

            nr_n = [0]

            def norm_rope(bk, gcol, half, dst, dst_res):
                i = nr_n[0] % 2
                j3 = nr_n[0] % 3
                nr_n[0] += 1
                hs = slice(half * 512, (half + 1) * 512)
                act(sq2[j3], banks[bk][:], AF.Square, [bres[bk]], [sq2_res[j3]])
                yield
                mm(banks[6][:], onesb[:], sq2[j3], True, True, [sq2_res[j3], r_ones], [bres[6]], signal=True)
                act(rs2[i], banks[6][:], AF.Ln, [bres[6]], [rs2_res[i]], bias=EPS, scale=1.0 / 128)
                act(rs2[i], rs2[i], AF.Exp, [rs2_res[i]], [rs2_res[i]], scale=-0.5)
                stt(qn2[j3], banks[bk][:], gq[:, gcol:gcol + 1], rs2[i], ALU.mult, ALU.mult,
                    [bres[bk], r_der, rs2_res[i]], [qn2_res[j3]])
                yield
                mm(banks[7][:], rpermb[:], qn2[j3], True, True, [qn2_res[j3], r_rp], [bres[7]], signal=True)
                tt("dve", t1[i], qn2[j3], cosT[:, hs], ALU.mult, [qn2_res[j3], r_rope], [t1_res[i]])
                tt("dve", t2[i], banks[7][:], sinT[:, hs], ALU.mult, [bres[7], r_rope2], [t2_res[i]])
                tt("dve", dst, t1[i], t2[i], ALU.add, [t1_res[i], t2_res[i]],
                   dst_res if isinstance(dst_res, list) else [dst_res])

            def run_pipeline(tiles):
                gens = []

                def fin_(g):
                    for _ in g:
                        pass
                for i_, (accf, args) in enumerate(tiles):
                    bk_ = accf()
                    g = norm_rope(bk_, *args)
                    next(g)
                    gens.append(g)
                    if i_ >= 1:
                        next(gens[i_ - 1])
                    if i_ >= 2:
                        fin_(gens[i_ - 2])
                n_ = len(tiles)
                if n_ >= 1:
                    next(gens[n_ - 1])
                if n_ >= 2:
                    fin_(gens[n_ - 2])
                if n_ >= 1:
                    fin_(gens[n_ - 1])

            P.dma("sp", lambda e: e.dma_start(out=ckst, in_=ck_d.rearrange("(kb p) n -> p kb n", p=128)),
                  writes=[ckst_res])
            for vs in range(2):
                s, sv = inproj_slot(2560 + vs * 256)
                for tb in range(8):
                    bk = 4 + (tb % 2)
                    half = tb // 4
                    for k in range(16):
                        mm(banks[bk][:, 0:256], hT[:, k, tb * 128:(tb + 1) * 128], sv[:, k, :], k == 0, k == 15,
                           [ring_res[s], hT_res[k][half]], [bres[bk]], signal=(k == 15))
                    cp("act", Vb[:, 4 + tb, vs * 256:(vs + 1) * 256], banks[bk][:, 0:256], [bres[bk]],
                       [V_res[4 + tb], bres[bk]])
                    vi = tb % 2
                    cp("dve", vst[vi], banks[bk][:, 0:256], [bres[bk]], [vst_res[vi]])
                    rr = Res()
                    P.dma("sp", lambda e, tb=tb, vs=vs, vi=vi: e.dma_start(
                        out=v_o[tb * 128:(tb + 1) * 128, vs * 256:(vs + 1) * 256], in_=vst[vi]),
                        reads=[vst_res[vi]], writes=[rr])
                    finals.append(rr)
            chk(20)
            P.dma("pool", lambda e: e.dma_start(out=Vb[:, 0:4, :], in_=cv_d.rearrange("(kb p) n -> p kb n", p=128)),
                  writes=V_res[0:4])
            for kb in range(4):
                bk = 4 + (kb % 2)
                for kvh in range(4):
                    tr(banks[bk][:, kvh * 128:(kvh + 1) * 128], ckst[:, kb, kvh * 128:(kvh + 1) * 128], identf[:],
                       [ckst_res, r_const], [bres[bk]], signal=(kvh == 3))
                cp("act", kT[:, :, kb * 128:(kb + 1) * 128], v3(banks[bk][:], 4), [bres[bk]],
                   [kT_res[kvh][kb] for kvh in range(4)])
            chk(21)
            deadck = retire([ckst_res])
            bg_acc["r"] = [Res(deps=deadck) for _ in range(2)]
            ktiles = []
            for ks in range(2):
                for mcl in range(2):
                    kvh = 2 * ks + mcl
                    for half in range(2):
                        def accf(ks=ks, mcl=mcl, half=half, first=(mcl == 0 and half == 0), cell=[None]):
                            if first:
                                run_bg_half(5)
                                kslot[ks] = inproj_slot(2048 + ks * 256)
                            s_, sv_ = kslot[ks]
                            return acc_fm(sv_, s_, mcl, half, hT_res)
                        ktiles.append((accf, (1, half, kT[:, kvh, 512 + half * 512:512 + (half + 1) * 512],
                                              [kT_res[kvh][4 + 4 * half + j] for j in range(4)])))
            kslot = {}
            run_pipeline(ktiles)
            chk(22)
            for tb in range(8):
                bk = 4 + (tb % 2)
                pb = banks[bk][:].bitcast(BF16)
                for kvh in range(4):
                    tr(pb[:, kvh * 128:(kvh + 1) * 128], kT[:, kvh, 512 + tb * 128:512 + (tb + 1) * 128], identb[:],
                       [kT_res[kvh][4 + tb], r_cb], [bres[bk]], signal=(kvh == 3))
                i = tb % 2
                cp("act", t1[i], pb[:, 0:512], [bres[bk]], [t1_res[i]])
                rr = Res()
                P.dma("sp", lambda e, tb=tb, i=i: e.dma_start(out=k_o[tb * 128:(tb + 1) * 128, :], in_=t1[i]),
                      reads=[t1_res[i]], writes=[rr])
                finals.append(rr)
            chk(23)
            for gs in range(8):
                if gs < 6:
                    run_bg_half(5)
                else:
                    while bg_pending:
                        bg_pending.pop(0)()
                s, sv = inproj_slot(3072 + gs * 256)
                for mcl in range(2):
                    c = 2 * gs + mcl
                    for half in range(2):
                        hs = slice(half * 512, (half + 1) * 512)
                        bk = acc_fm(sv, s, mcl, half, hT_res)
                        i = (2 * c + half) % 2
                        act(tg[i], banks[bk][:], AF.Tanh, [bres[bk]], [tg_res[i]], scale=0.5)
                        stt(gT[:, c, hs], tg[i], 1.0, banks[bk][:], ALU.add, ALU.mult, [tg_res[i], bres[bk]],
                            [gT_res[c][half]])
            chk(24)
            qtiles = []
            qslot = {}
            for qs in range(8):
                for mcl in range(2):
                    h = 2 * qs + mcl
                    for half in range(2):
                        def accf(qs=qs, mcl=mcl, half=half, first=(mcl == 0 and half == 0)):
                            if first:
                                qslot[qs] = inproj_slot(qs * 256)
                            s_, sv_ = qslot[qs]
                            return acc_fm(sv_, s_, mcl, half, hT_res)
                        qtiles.append((accf, (0, half, qT[:, h, half * 512:(half + 1) * 512], qT_res[h][half])))
            run_pipeline(qtiles)

            chk(2)
            dead2 = retire([r for row in hT_res for r in row])
            a2b = A2[:].bitcast(BF16)
            NPT = 8
            Pt = [a2b[:, i * 512:(i + 1) * 512] for i in range(NPT)]
            Pt_res = [Res(deps=dead2) for _ in range(NPT)]
            rden = [A2[:, 2048 + i * 512:2048 + (i + 1) * 512] for i in range(2)]
            rden_res = [Res(deps=dead2) for _ in range(2)]
            otmp = [A2[:, 3072 + i * 512:3072 + (i + 1) * 512] for i in range(2)]
            otmp_res = [Res(deps=dead2) for _ in range(2)]
            items = [(h, c, kb) for h in range(16) for c in range(2) for kb in range(12)]
            SB = [0, 1, 6, 7]

            def emit_S(i):
                h, c, kb = items[i]
                kvh = h // 4
                bk = SB[i % 4]
                mm(banks[bk][:], kT[:, kvh, kb * 128:(kb + 1) * 128], qT[:, h, c * 512:(c + 1) * 512], True, True,
                   [kT_res[kvh][kb], qT_res[h][c]], [bres[bk]], signal=True)

            def emit_exp(i):
                h, c, kb = items[i]
                bk = SB[i % 4]
                pi = i % NPT
                act(Pt[pi], banks[bk][:], AF.Exp, [bres[bk]], [Pt_res[pi]])
                col = kb * 4 + 2 * c
                diag = kb >= 4 and ((kb - 4) // 4 == c)
                if diag:
                    for hh in range(2):
                        ts("dve", Pt[pi][:, hh * 256:(hh + 1) * 256], Pt[pi][:, hh * 256:(hh + 1) * 256],
                           maskb[:, col + hh:col + hh + 1], None, ALU.mult, None, [Pt_res[pi], r_maskb], [Pt_res[pi]])
                else:
                    ts("dve", Pt[pi], Pt[pi], maskb[:, col:col + 1], None, ALU.mult, None, [Pt_res[pi], r_maskb],
                       [Pt_res[pi]])

            def emit_PV(i):
                h, c, kb = items[i]
                kvh = h // 4
                pi = i % NPT
                hc = (2 * h + c) % 2
                ob = 2 + hc
                db = 4 + hc
                mm(banks[ob][:], Vb[:, kb, kvh * 128:(kvh + 1) * 128], Pt[pi], kb == 0, kb == 11,
                   [V_res[kb], Pt_res[pi]], [bres[ob]], signal=(kb == 11))
                mm(banks[db][:], onesb[:], Pt[pi], kb == 0, kb == 11, [Pt_res[pi], r_ones], [bres[db]],
                   signal=(kb == 11))

            def emit_epi(h, c):
                hc = (2 * h + c) % 2
                ob = 2 + hc
                db = 4 + hc
                cs = slice(c * 512, (c + 1) * 512)
                act(rden[hc], banks[db][:], AF.Ln, [bres[db]], [rden_res[hc]])
                act(rden[hc], rden[hc], AF.Exp, [rden_res[hc]], [rden_res[hc]], scale=-1.0)
                tt("dve", otmp[hc], banks[ob][:], rden[hc], ALU.mult, [bres[ob], rden_res[hc]], [otmp_res[hc]])
                stt(qT[:, h, cs], otmp[hc], 0.5, gT[:, h, cs], ALU.mult, ALU.mult, [otmp_res[hc], gT_res[h][c]],
                    [qT_res[h][c]])

            LOOK = 3
            pre4 = {}
            cur_job = [None]
            att_jobs = list(bg_jobs[:4])
            del bg_jobs[:4]
            att_acc = [A2[:, 4096 + n * 512:4096 + (n + 1) * 512] for n in range(8)]
            att_acc_res = [Res(deps=dead2) for _ in range(8)]
            att_final = []
            att_slots = {}
            att_n = [0]
            assert len(att_jobs) <= 8
            if att_jobs:
                att_slots[0] = bg_load(*att_jobs[0])
            pend_epi = []
            for i0 in range(LOOK):
                emit_S(i0)
            for i in range(len(items)):
                h, c, kb = items[i]
                if i + LOOK < len(items):
                    emit_S(i + LOOK)
                emit_exp(i)
                emit_PV(i)
                if pend_epi and kb == 3:
                    emit_epi(*pend_epi.pop(0))
                if i % 6 == 5:
                    if cur_job[0] is None and att_n[0] < len(att_jobs):
                        n_ = att_n[0]
                        att_n[0] += 1
                        l_, g_ = att_jobs[n_]
                        cur_job[0] = adaln_group_dve(l_, g_, 7, acc=att_acc[n_], acc_res=att_acc_res[n_],
                                                     slots=att_slots[n_], defer=att_final)
                        if n_ + 1 < len(att_jobs):
                            att_slots[n_ + 1] = bg_load(*att_jobs[n_ + 1])
                        else:
                            pre4[0] = oproj_load(woa_d, 0)
                            pre4[1] = oproj_load(woa_d, 1)
                    if cur_job[0] is not None:
                        try:
                            next(cur_job[0])
                        except StopIteration:
                            cur_job[0] = None
                if kb == 11:
                    pend_epi.append((h, c))
            while pend_epi:
                emit_epi(*pend_epi.pop(0))
            if cur_job[0] is not None:
                for _ in cur_job[0]:
                    pass
            while att_n[0] < len(att_jobs):
                n_ = att_n[0]
                att_n[0] += 1
                l_, g_ = att_jobs[n_]
                for _ in adaln_group_dve(l_, g_, 7, acc=att_acc[n_], acc_res=att_acc_res[n_],
                                         slots=att_slots.get(n_), defer=att_final):
                    pass
            while bg_pending:
                bg_pending.pop(0)()
            for f_ in att_final:
                f_()

            chk(3)
            dead1 = retire([r for row in gT_res for r in row] + [r_rope, r_rope2] + sq2_res + qn2_res + rs2_res + t1_res +
                           t2_res + tg_res + vst_res)
            mT = v3(A1[:], 16)
            m_res = [[Res(deps=dead1) for _ in range(2)] for _ in range(16)]
            a2c = A2[:, 4096:8192]
            a2cb = a2c.bitcast(BF16)
            sq4 = [a2cb[:, i * 512:(i + 1) * 512] for i in range(3)]
            dead_att = retire(att_acc_res)
            sq4_res = [Res(deps=dead2 + dead_att) for _ in range(3)]
            rstd4 = [a2c[:, 1024 + i * 512:1024 + (i + 1) * 512] for i in range(2)]
            rstd4_res = [Res(deps=dead2 + dead_att) for _ in range(2)]
            p4_left = [4 if depth > 1 else 0]

            def p4_hook(sidx):
                if p4_left[0] > 0 or bg_cur[0] is not None:
                    if bg_cur[0] is None:
                        p4_left[0] -= 1
                    run_bg_half(7)
                else:
                    while bg_pending:
                        bg_pending.pop(0)()
            outproj(woa_d, qT, qT_res, mT, m_res, sq4, sq4_res, rstd4, rstd4_res, before_slot=p4_hook,
                    pre_slots=pre4)
            while bg_cur[0] is not None:
                run_bg_half(7)
            while bg_pending:
                bg_pending.pop(0)()
            dead4b = retire([r for row in kT_res for r in row] + V_res + [ckst_res] + bg_acc["r"])
            d0T = v3(A4[:].bitcast(BF16), 16)
            d0_res = [[Res(deps=dead4b) for _ in range(2)] for _ in range(16)]
            for half in range(2):
                hs = slice(half * 512, (half + 1) * 512)
                for c in range(16):
                    stt(d0T[:, c, hs], mT[:, c, hs], ggv[0][:, c:c + 1], rstd4[half], ALU.mult, ALU.mult,
                        [m_res[c][half], r_modg[0], rstd4_res[half]], [d0_res[c][half]])

            if depth == 1:
                deadA2 = retire(Pt_res + rden_res + otmp_res + sq4_res + rstd4_res + att_acc_res)
                xs7 = [A2[:, i * 2048:(i + 1) * 2048] for i in range(2)]
                xs7_res = [Res(deps=deadA2) for _ in range(2)]
                os7 = [A2[:, 4096 + i * 2048:4096 + (i + 1) * 2048] for i in range(2)]
                os7_res = [Res(deps=deadA2) for _ in range(2)]
                for tb in range(8):
                    si = tb % 2
                    half = tb // 4
                    P.dma("sp", lambda e, tb=tb, si=si: e.dma_start(out=xs7[si], in_=x_d[tb * 128:(tb + 1) * 128, :]),
                          writes=[xs7_res[si]])
                    for q in range(2):
                        pb = banks[q][:].bitcast(BF16)
                        for j in range(8):
                            c = 8 * q + j
                            tr(pb[:, j * 128:(j + 1) * 128], d0T[:, c, tb * 128:(tb + 1) * 128], identb[:],
                               [d0_res[c][half], r_cb], [bres[q]], signal=(j == 7))
                        tt("dve", os7[si][:, q * 1024:(q + 1) * 1024], pb[:, 0:1024], xs7[si][:, q * 1024:(q + 1) * 1024],
                           ALU.add, [bres[q], xs7_res[si]], [os7_res[si]])
                    rr = Res()
                    P.dma("sp", lambda e, tb=tb, si=si: e.dma_start(out=y_o[tb * 128:(tb + 1) * 128, :], in_=os7[si]),
                          reads=[os7_res[si]], writes=[rr])
                    finals.append(rr)
                rr = Res()
                P.op("dve", lambda e: e.memset(fino[:], 0.0), writes=[r_fin])
                P.dma("sp", lambda e: e.dma_start(out=st_o, in_=fino[:]), reads=[r_fin], writes=[rr])
                finals.append(rr)
                raise _Stop()

            gwt = [sb("gwt%d" % i, [128, 2048], BF16) for i in range(2)]
            gwt_res = [[Res() for _ in range(4)] for _ in range(2)]
            xslot = {}

            def load_slot(j, which):
                col0 = (0 if which == 0 else 2048) + j * 256
                src = wil_d[:, col0:col0 + 256].rearrange("(k p) n -> p k n", p=128)
                s_ = wload(lambda r: v3(r[:], 16), src)
                xslot[(j, which)] = (s_, v3(RING[s_][:], 16))

            def load_gw(j):
                gi = j % 2
                gvj = gwt[gi][:].rearrange("p (m k n) -> p m k n", m=4, k=2)
                for m in range(4):
                    d__ = m // 2
                    wsrc = (wra_d if m % 2 == 0 else wrx_d)[d__, j].rearrange("(k p) n -> p k n", p=128)
                    P.dma("pool", lambda e, gvj=gvj, m=m, wsrc=wsrc: e.dma_start(out=gvj[:, m], in_=wsrc),
                          writes=[gwt_res[gi][m]])

            load_slot(0, 0)
            load_gw(0)
            load_slot(0, 1)
            chk(4)
            deadm = retire([r for row in m_res for r in row])
            y1T = v3(A1[:], 16)
            y1_res = [[Res(deps=deadm) for _ in range(2)] for _ in range(16)]
            dead3b = retire([r for row in qT_res for r in row])
            xs5 = [A3[:, 0:2048], A3[:, 2048:4096]]
            xs5_res = [Res(deps=dead3b), Res(deps=dead3b)]
            sq5 = [A3[:, 4096:8192].bitcast(BF16)[:, i * 512:(i + 1) * 512] for i in range(3)]
            sq5_res = [Res(deps=dead3b) for _ in range(3)]
            rstd5 = [A3[:, 5120 + i * 512:5120 + (i + 1) * 512] for i in range(2)]
            rstd5_res = [Res(deps=dead3b) for _ in range(2)]
            tmp5 = [A3[:, 6144 + i * 512:6144 + (i + 1) * 512] for i in range(2)]
            tmp5_res = [Res(deps=dead3b) for _ in range(2)]

            def consume_x5(tb, q):
                half = tb // 4
                bs = slice(tb * 128, (tb + 1) * 128)
                tt("dve", y1T[:, 4 * q:4 * q + 4, bs], v3(banks[q][:], 4), d0T[:, 4 * q:4 * q + 4, bs], ALU.add,
                   [bres[q]] + [d0_res[4 * q + j][half] for j in range(4)], [y1_res[4 * q + j][half] for j in range(4)])
            load_xT(xs5, xs5_res, consume_x5)
            deadA2 = retire(Pt_res + rden_res + otmp_res + sq4_res + rstd4_res + att_acc_res)
            h1T = v3(A2[:].bitcast(BF16), 16)
            h1_res = [[Res(deps=deadA2) for _ in range(2)] for _ in range(16)]
            prenorm(y1T, y1_res, 1, h1T, h1_res, sq5, sq5_res, rstd5, rstd5_res, tmp5, tmp5_res)

            chk(5)
            deady = retire([r for row in y1_res for r in row])
            dead3c = retire(xs5_res + sq5_res + rstd5_res + tmp5_res)
            ogT = v3(A3[:].bitcast(BF16), 16)
            og_res = [[Res(deps=dead3c) for _ in range(2)] for _ in range(16)]
            XW = 1028
            xbp = [A1[:, i * XW:(i + 1) * XW] for i in range(2)]
            xbp_res = [Res(deps=deady) for _ in range(2)]
            xc = [A1[:, 2064 + i * 1024:2064 + (i + 1) * 1024] for i in range(2)]
            xc_res = [Res(deps=deady) for _ in range(2)]
            xcb = [A1[:, 4112 + i * 512:4112 + (i + 1) * 512].bitcast(BF16) for i in range(2)]
            xcb_res = [Res(deps=deady) for _ in range(2)]
            g2 = [A1[:, 5136 + i * 512:5136 + (i + 1) * 512].bitcast(BF16) for i in range(2)]
            g2_res = [Res(deps=deady) for _ in range(2)]
            hf = A1[:, 6160:7184]
            hb = A1[:, 7184:8208]
            hf_res = Res(deps=deady)
            hb_res = Res(deps=deady)
            trt = [A1[:, 8208 + i * 512:8208 + (i + 1) * 512] for i in range(3)]
            trt_res = [Res(deps=deady) for _ in range(3)]
            tit = [A1[:, 9744 + i * 256:9744 + (i + 1) * 256].bitcast(BF16) for i in range(4)]
            tit_res = [Res(deps=deady) for _ in range(4)]
            at = [A1[:, 10768 + i * 1024:10768 + (i + 1) * 1024] for i in range(2)]
            at_res = [Res(deps=deady) for _ in range(2)]
            sqt6 = [A1[:, 12816 + i * 512:12816 + (i + 1) * 512] for i in range(4)]
            sqt6_res = [Res(deps=deady) for _ in range(4)]
            P.op("dve", lambda e: e.memset(fin[:], 0.0), writes=[r_fin])

            cnt6 = [0]
            g2b = [g2, [A1[:, 14864 + i * 512:14864 + (i + 1) * 512].bitcast(BF16) for i in range(2)]]
            g2b_res = [g2_res, [Res(deps=deady) for _ in range(2)]]

            def emit_proj_part(j, part):
                mcl = part % 2
                c = 2 * j + mcl
                s_, sv = xslot[(j, part // 2)]
                for half in range(2):
                    hs = slice(half * 512, (half + 1) * 512)
                    bk = acc_n[0] % 4
                    acc_n[0] += 1
                    for k in range(16):
                        mm(banks[bk][:], sv[:, k, mcl * 128:(mcl + 1) * 128], h1T[:, k, hs], k == 0, k == 15,
                           [ring_res[s_], h1_res[k][half]], [bres[bk]], signal=(k == 15))
                    if part < 2:
                        cp("act", xbp[mcl][:, 1 + half * 512:1 + (half + 1) * 512], banks[bk][:], [bres[bk]],
                           [xbp_res[mcl]])
                    else:
                        i = cnt6[0] % 3
                        cnt6[0] += 1
                        act(trt[i], banks[bk][:], AF.Tanh, [bres[bk]], [trt_res[i]], scale=0.5)
                        stt(g2b[j % 2][mcl][:, hs], trt[i], 1.0, banks[bk][:], ALU.add, ALU.mult,
                            [trt_res[i], bres[bk]], [g2b_res[j % 2][mcl]])

            def emit_conv(j, mcl):
                ow = PP["convw"][0]
                ob_ = PP["convb"][0]
                if True:
                    c = 2 * j + mcl
                    ts("dve", xc[mcl], xbp[mcl][:, 0:1024], pp[:, ow + c:ow + c + 1], pp[:, ob_ + c:ob_ + c + 1],
                       ALU.mult, ALU.add, [xbp_res[mcl], r_pp], [xc_res[mcl]])
                    for jj in range(1, 4):
                        stt(xc[mcl], xbp[mcl][:, jj:jj + 1024], pp[:, ow + jj * 16 + c:ow + jj * 16 + c + 1],
                            xc[mcl], ALU.mult, ALU.add, [xbp_res[mcl], r_pp, xc_res[mcl]], [xc_res[mcl]])

                    def fix(dst_t0, src_t0, jj, mcl=mcl, c=c):
                        dv = xc[mcl][:, dst_t0:dst_t0 + 513:256]
                        sv_ = xbp[mcl][:, src_t0 + 1:src_t0 + 1 + 513:256]
                        stt(dv, sv_, wneg[:, jj * 16 + c:jj * 16 + c + 1], dv, ALU.mult, ALU.add,
                            [xbp_res[mcl], r_der, xc_res[mcl]], [xc_res[mcl]])
                    fix(256, 255, 0)
                    fix(255, 256, 2)
                    fix(255, 257, 3)
                    fix(254, 256, 3)

            def emit_xcb(j):
                for mcl in range(2):
                    cp("dve", xcb[mcl], xc[mcl], [xc_res[mcl]], [xcb_res[mcl]])

            def emit_group(j, g, mid=None):
                mcl = g // 2
                d_ = g % 2
                c = 2 * j + mcl
                gi = j % 2
                gv = gwt[gi][:].rearrange("p (m k n) -> p m k n", m=4, k=2)
                dc = d_ * 16 + c
                ai = d_
                ks = []
                for half in range(2):
                    hs = slice(half * 512, (half + 1) * 512)
                    bkr = 4 + (cnt6[0] % 2)
                    bki = 6 + (cnt6[0] % 2)
                    i3 = cnt6[0] % 3
                    i4 = cnt6[0] % 4
                    cnt6[0] += 1
                    ks.append((half, hs, i3, i4))
                    for kc in range(2):
                        mm(banks[bkr][:], gv[:, 2 * d_, kc, mcl * 128:(mcl + 1) * 128], xcb[kc][:, hs],
                           kc == 0, kc == 1, [gwt_res[gi][2 * d_], xcb_res[kc]], [bres[bkr]], signal=(kc == 1))
                    for kc in range(2):
                        mm(banks[bki][:], gv[:, 2 * d_ + 1, kc, mcl * 128:(mcl + 1) * 128], xcb[kc][:, hs],
                           kc == 0, kc == 1, [gwt_res[gi][2 * d_ + 1], xcb_res[kc]], [bres[bki]], signal=(kc == 1))
                    act(trt[i3], banks[bkr][:], AF.Tanh, [bres[bkr], r_der], [trt_res[i3]],
                        bias=hba[:, dc:dc + 1], scale=0.5)
                    act(tit[i4], banks[bki][:], AF.Tanh, [bres[bki], r_der], [tit_res[i4]],
                        bias=hbx[:, dc:dc + 1], scale=0.5)
                for (half, hs, i3, i4) in ks:
                    stt(tit[i4], tit[i4], 1.0, xc[mcl][:, hs], ALU.add, ALU.mult, [tit_res[i4], xc_res[mcl]],
                        [tit_res[i4]])
                if mid is not None:
                    mid()
                for (half, hs, i3, i4) in ks:
                    act(at[ai][:, hs], trt[i3], AF.Exp, [trt_res[i3], r_der], [at_res[ai]],
                        bias=hcl[:, dc:dc + 1], scale=hcl[:, dc:dc + 1])
                    act(sqt6[i4], trt[i3], AF.Exp, [trt_res[i3], r_der], [sqt6_res[i4]],
                        bias=cl[:, dc:dc + 1], scale=cl[:, dc:dc + 1])
                for (half, hs, i3, i4) in ks:
                    act(sqt6[i4], sqt6[i4], AF.Sqrt, [sqt6_res[i4]], [sqt6_res[i4]], bias=0.25, scale=-0.25)
                for (half, hs, i3, i4) in ks:
                    tt("dve", sqt6[i4], sqt6[i4], tit[i4], ALU.mult, [sqt6_res[i4], tit_res[i4]], [sqt6_res[i4]])
                if d_ == 0:
                    av = at[ai][:, 256:769:256]
                else:
                    av = at[ai][:, 255:768:256]
                ts("dve", av, av, flags[:, 0:1], None, ALU.mult, None, [at_res[ai], r_flags], [at_res[ai]])
                (_, hs0, _, k0), (_, hs1, _, k1) = ks
                fv = fin[:].rearrange("p (s r) -> p s r", s=4)[:, :, dc]
                if d_ == 0:
                    P.op("dve", lambda e: e.tensor_tensor_scan(
                        out=hf[:, 0:512], data0=at[ai][:, 0:512], data1=sqt6[k0], initial=h0t[:, dc:dc + 1],
                        op0=ALU.mult, op1=ALU.add), reads=[at_res[ai], sqt6_res[k0], r_h0], writes=[hf_res])
                    P.op("dve", lambda e: e.tensor_tensor_scan(
                        out=hf[:, 512:1024], data0=at[ai][:, 512:1024], data1=sqt6[k1], initial=hf[:, 511:512],
                        op0=ALU.mult, op1=ALU.add), reads=[at_res[ai], sqt6_res[k1], hf_res], writes=[hf_res])
                    cp("dve", fv, hf[:, 255:1024:256], [hf_res], [r_fin])
                else:
                    P.op("dve", lambda e: e.tensor_tensor_scan(
                        out=hb[:, 512:1024][:, ::-1], data0=at[ai][:, 512:1024][:, ::-1], data1=sqt6[k1][:, ::-1],
                        initial=h0t[:, dc:dc + 1], op0=ALU.mult, op1=ALU.add),
                        reads=[at_res[ai], sqt6_res[k1], r_h0], writes=[hb_res])
                    P.op("dve", lambda e: e.tensor_tensor_scan(
                        out=hb[:, 0:512][:, ::-1], data0=at[ai][:, 0:512][:, ::-1], data1=sqt6[k0][:, ::-1],
                        initial=hb[:, 512:513], op0=ALU.mult, op1=ALU.add),
                        reads=[at_res[ai], sqt6_res[k0], hb_res], writes=[hb_res])
                    cp("dve", fv, hb[:, 0:1024:256], [hb_res], [r_fin])
                    tt("dve", hf, hf, hb, ALU.add, [hf_res, hb_res], [hf_res])
                    stt(ogT[:, c, :], hf, 0.5, g2b[j % 2][mcl], ALU.mult, ALU.mult, [hf_res, g2b_res[j % 2][mcl]],
                        [og_res[c][0], og_res[c][1]])

            for mcl_ in range(2):
                P.op("dve", lambda e, mcl_=mcl_: e.memset(xbp[mcl_][:, 0:1], 0.0), writes=[xbp_res[mcl_]])
                P.op("dve", lambda e, mcl_=mcl_: e.memset(xbp[mcl_][:, 1025:1028], 0.0), writes=[xbp_res[mcl_]])
            for j in range(8):
                if j + 1 < 8:
                    load_slot(j + 1, 0)
                emit_proj_part(j, 0)
                if j >= 1:
                    emit_group(j - 1, 0)
                if j >= 1:
                    emit_group(j - 1, 1, mid=lambda j=j: emit_conv(j, 0))
                emit_proj_part(j, 1)
                if j == 0:
                    emit_conv(j, 0)
                if j + 1 < 8:
                    load_slot(j + 1, 1)
                if j >= 1:
                    emit_group(j - 1, 2)
                emit_proj_part(j, 2)
                if j >= 1:
                    emit_group(j - 1, 3, mid=lambda j=j: (emit_conv(j, 1), emit_xcb(j)))
                emit_proj_part(j, 3)
                if j == 0:
                    emit_conv(j, 1)
                    emit_xcb(j)
                if j + 1 < 8:
                    load_gw(j + 1)
            pre7 = {0: oproj_load(wol_d, 0)}
            for g in range(4):
                emit_group(7, g)

            tr(banks[7][:, 0:128], fin[:], identf[:], [r_fin, r_const], [bres[7]], signal=True)
            r_fino = Res()
            cp("act", fino[:], banks[7][:, 0:128], [bres[7]], [r_fino])
            rr = Res()
            P.dma("sp", lambda e: e.dma_start(out=st_o, in_=fino[:]), reads=[r_fino], writes=[rr])
            finals.append(rr)

            chk(6)
            deadw = retire(xbp_res + xc_res + xcb_res + g2_res + [hf_res, hb_res] + trt_res + tit_res + at_res +
                           sqt6_res + g2b_res[1])
            m1T = v3(A1[:], 16)
            m1_res = [[Res(deps=deadw) for _ in range(2)] for _ in range(16)]
            deadh1 = retire([r for row in h1_res for r in row])
            xs7 = [A2[:, i * 2048:(i + 1) * 2048] for i in range(2)]
            xs7_res = [Res(deps=deadh1) for _ in range(2)]
            sq7 = [A2[:, 6144:7168].bitcast(BF16)[:, i * 512:(i + 1) * 512] for i in range(3)]
            sq7_res = [Res(deps=deadh1) for _ in range(3)]
            rstd7 = [A2[:, 7168 + i * 512:7168 + (i + 1) * 512] for i in range(2)]
            rstd7_res = [Res(deps=deadh1) for _ in range(2)]
            p7_acc = [A2[:, 4096 + n * 512:4096 + (n + 1) * 512] for n in range(4)]
            p7_acc_res = [Res(deps=deadh1) for _ in range(4)]
            p7_n = [0]

            def p7_hook(sidx):
                while bg_pending:
                    bg_pending.pop(0)()
                if bg_jobs and sidx < 6:
                    l_, g_ = bg_jobs.pop(0)
                    n_ = p7_n[0]
                    p7_n[0] += 1
                    for _ in adaln_group_dve(l_, g_, 7, acc=p7_acc[n_ % 4], acc_res=p7_acc_res[n_ % 4]):
                        pass
            outproj(wol_d, ogT, og_res, m1T, m1_res, sq7, sq7_res, rstd7, rstd7_res, before_slot=p7_hook,
                    pre_slots=pre7)
            while bg_jobs:
                p7_hook(0)
            while bg_pending:
                bg_pending.pop(0)()
            def ew7(c, half):
                hs = slice(half * 512, (half + 1) * 512)
                stt(m1T[:, c, hs], m1T[:, c, hs], ggv[1][:, c:c + 1], rstd7[half], ALU.mult, ALU.mult,
                    [m1_res[c][half], r_modg[1], rstd7_res[half]], [m1_res[c][half]])
                tt("dve", m1T[:, c, hs], m1T[:, c, hs], d0T[:, c, hs], ALU.add,
                   [m1_res[c][half], d0_res[c][half]], [m1_res[c][half]])

            deadog = retire([r for row in og_res for r in row])
            os7 = [A3[:, i * 2048:(i + 1) * 2048] for i in range(3)]
            os7_res = [Res(deps=deadog) for _ in range(3)]

            def load7(tb):
                si = tb % 2
                P.dma("sp", lambda e: e.dma_start(out=xs7[si], in_=x_d[tb * 128:(tb + 1) * 128, :]),
                      writes=[xs7_res[si]])

            load7(0)
            load7(1)
            for c in range(16):
                ew7(c, 0)
            for tb in range(8):
                si = tb % 2
                oi = tb % 3
                half = tb // 4
                for q in range(4):
                    for j in range(4):
                        c = 4 * q + j
                        tr(banks[q][:, j * 128:(j + 1) * 128], m1T[:, c, tb * 128:(tb + 1) * 128], identf[:],
                           [m1_res[c][half], r_const], [bres[q]], signal=(j == 3))
                    tt("dve", os7[oi][:, q * 512:(q + 1) * 512], banks[q][:], xs7[si][:, q * 512:(q + 1) * 512],
                       ALU.add, [bres[q], xs7_res[si]], [os7_res[oi]])
                if tb + 2 < 8:
                    load7(tb + 2)
                rr = Res()
                P.dma("sp", lambda e, tb=tb, oi=oi: e.dma_start(out=y_o[tb * 128:(tb + 1) * 128, :], in_=os7[oi]),
                      reads=[os7_res[oi]], writes=[rr])
                finals.append(rr)
                if tb < 4:
                    for c in range(4 * tb, 4 * tb + 4):
                        ew7(c, 1)

        except _Stop:
            pass
        P.emit(nc, final_waits=finals)
    return nc


def _pcol(v):
    v = np.asarray(v, np.float32).reshape(-1, 128)
    return np.ascontiguousarray(v.T)


def _rope_tables(n):
    grid_w = 64
    rows = n // grid_w
    row = np.repeat(np.arange(rows, dtype=np.float32), grid_w)
    col = np.tile(np.arange(grid_w, dtype=np.float32), rows)
    inv = (np.float32(10000.0) ** (-np.arange(0, 64, 2, dtype=np.float32) / np.float32(64))).astype(np.float32)
    ar = row[:, None] * inv
    ac = col[:, None] * inv
    ang = np.concatenate([ar, ar, ac, ac], axis=-1).astype(np.float32)
    return np.cos(ang).astype(np.float32), np.sin(ang).astype(np.float32)


def _prep_inputs(x_prompt, x_sample, c, cache_k, cache_v, state_lru, c_ctx, w_mod, b_mod, g_pre, g_post,
                 w_in_attn, q_norm, k_norm, w_out_attn, w_in_lru, conv_w, conv_b,
                 w_rg_a, b_rg_a, w_rg_x, b_rg_x, lru_lambda, w_out_lru):
    f = lambda a: np.ascontiguousarray(np.asarray(a, dtype=np.float32))
    ident = np.eye(128, dtype=np.float32)
    rperm = np.zeros((128, 128), np.float32)
    for dp in range(128):
        base = (dp // 64) * 64
        r = dp - base
        if r < 32:
            rperm[base + r + 32, dp] = -1.0
        else:
            rperm[base + r - 32, dp] = 1.0
    pp = np.zeros((128, NPP), np.float32)

    def put(name, arr):
        o, w = PP[name]
        assert arr.shape == (128, w), (name, arr.shape)
        pp[:, o:o + w] = arr
    put("bmod0", _pcol(b_mod[0]))
    put("bmod1", _pcol(b_mod[1]))
    put("gpre0", _pcol(g_pre[0]))
    put("gpre1", _pcol(g_pre[1]))
    put("gpost0", _pcol(g_post[0]))
    put("gpost1", _pcol(g_post[1]))
    put("qn", _pcol(q_norm[0]))
    put("kn", _pcol(k_norm[0]))
    put("convw", np.concatenate([_pcol(conv_w[0, j]) for j in range(4)], axis=1))
    put("convb", _pcol(conv_b[0]))
    put("ba", np.concatenate([_pcol(b_rg_a[0, d]) for d in range(2)], axis=1))
    put("bx", np.concatenate([_pcol(b_rg_x[0, d]) for d in range(2)], axis=1))
    put("lam", np.concatenate([_pcol(lru_lambda[0, d]) for d in range(2)], axis=1))

    cos_s, sin_s = _rope_tables(1024)
    shared = {
        "ident": ident, "rperm": rperm, "pp": pp,
        "w_mod": f(w_mod), "w_in_attn": f(w_in_attn[0]), "w_out_attn": f(w_out_attn[0]),
        "w_in_lru": f(w_in_lru[0]), "w_out_lru": f(w_out_lru[0]), "w_rg_a": f(w_rg_a[0]), "w_rg_x": f(w_rg_x[0]),
    }
    in_maps = []
    for core in range(8):
        m = dict(shared)
        if core < 4:
            m["x"] = f(x_prompt[4 * core:4 * core + 4]).reshape(NT, D)
            m["cond"] = _pcol(c_ctx)
            m["ck"] = np.zeros((512, 512), np.float32)
            m["cv"] = np.zeros((512, 512), np.float32)
            m["h0"] = np.zeros((128, 32), np.float32)
            fl = np.zeros((128, 8), np.float32)
            fl[:, 0] = 0.0
            fl[:, 1] = -1.0
            mb = np.zeros((12, 4), np.float32)
            for kb in range(4, 12):
                mb[kb, (kb - 4) // 2] = 1.0
            m["cosT"] = np.ones((128, NT), np.float32)
            m["sinT"] = np.zeros((128, NT), np.float32)
        else:
            b = core - 4
            m["x"] = f(x_sample[b])
            m["cond"] = _pcol(c[b])
            m["ck"] = f(cache_k[b, 0]).reshape(512, 512)
            m["cv"] = f(cache_v[b, 0]).reshape(512, 512)
            m["h0"] = np.concatenate([_pcol(state_lru[b, 0, d]) for d in range(2)], axis=1)
            fl = np.zeros((128, 8), np.float32)
            fl[:, 0] = 1.0
            fl[:, 1] = 0.0
            mb = np.ones((12, 4), np.float32)
            m["cosT"] = np.ascontiguousarray(cos_s.T)
            m["sinT"] = np.ascontiguousarray(sin_s.T)
        m["flags"] = fl
        m["maskb"] = np.ascontiguousarray(np.broadcast_to(mb.reshape(1, 48), (128, 48))).astype(np.float32)
        in_maps.append(m)
    return in_maps


_CACHE = {}


def kernel(**inputs):
    in_maps = _prep_inputs(**inputs)
    if "nc" not in _CACHE:
        _CACHE["nc"] = build_program(2)
    nc = _CACHE["nc"]
    res = run_bass_kernel_spmd(nc, in_maps, core_ids=list(range(8)))
    r = res.results
    y_p = np.stack([r[i]["y_out"] for i in range(4)]).reshape(16, 256, D).astype(np.float32)
    y_s = np.stack([r[4 + i]["y_out"] for i in range(4)]).reshape(4, 1024, D).astype(np.float32)
    nk = np.stack([r[i]["k_out"] for i in range(4)]).reshape(16, 1, 256, 4, 128).astype(np.float32)
    nv = np.stack([r[i]["v_out"] for i in range(4)]).reshape(16, 1, 256, 4, 128).astype(np.float32)
    ns = np.stack([r[i]["st_out"] for i in range(4)]).reshape(4, 4, 2, 16 * 128).reshape(16, 1, 2, D).astype(np.float32)
    return (y_p, y_s, nk, nv, ns)
```

```python
import contextlib
import numpy as np
import concourse.bass as bass
import concourse.mybir as mybir
from concourse.bass_utils import run_bass_kernel_spmd

F32 = mybir.dt.float32
BF16 = mybir.dt.bfloat16
AF = mybir.ActivationFunctionType
ALU = mybir.AluOpType

NT = 1024
D = 2048
KC = 16
EPS = 1e-6
NEG = -30000.0

PP = {}
_o = 0
for _n, _w in [("bmod0", 48), ("bmod1", 48), ("gpre0", 16), ("gpre1", 16), ("gpost0", 16), ("gpost1", 16),
               ("qn", 1), ("kn", 1), ("convw", 64), ("convb", 16), ("ba", 32), ("bx", 32), ("lam", 32)]:
    PP[_n] = (_o, _w)
    _o += _w
NPP = _o


class Res:
    __slots__ = ("name", "w", "rd")

    def __init__(self, name="", deps=None):
        self.name = name
        self.w = None
        self.rd = list(deps) if deps else []


def retire(rs):
    out = []
    for r in rs:
        if r.w is not None:
            out.append(r.w)
        out.extend(r.rd)
    return list(dict.fromkeys(out))


class Op:
    __slots__ = ("fn", "deps", "signal", "dma", "dma_sem", "dma_val", "pre")

    def __init__(self, fn, deps, signal):
        self.fn = fn
        self.deps = deps
        self.signal = signal
        self.dma = False
        self.dma_sem = None
        self.dma_val = 0
        self.pre = None


class Plan:
    ENGS = ("pe", "act", "dve", "pool", "sp")
    NDMASEM = {"sp": 12, "pool": 12}

    def __init__(self):
        self.ops = {e: [] for e in self.ENGS}
        self.dma_count = {q: 0 for q in self.NDMASEM}

    def _deps(self, eng, reads, writes):
        deps = []
        for r in reads:
            if r.w is not None:
                deps.append(r.w)
        for w in writes:
            if w.w is not None:
                deps.append(w.w)
            deps.extend(w.rd)
        out = []
        for d in deps:
            if d[0] == "eng" and d[1] == eng and eng == "pe":
                continue
            out.append(d)
        return out

    def op(self, eng, fn, reads=(), writes=(), signal=True):
        deps = self._deps(eng, reads, writes)
        idx = len(self.ops[eng])
        self.ops[eng].append(Op(fn, deps, signal))
        tag = ("eng", eng, idx)
        for r in reads:
            r.rd.append(tag)
        for w in writes:
            w.w = tag
            w.rd = []
        return tag

    def dma(self, q, fn, reads=(), writes=()):
        deps = self._deps(q, reads, writes)
        n = self.dma_count[q]
        self.dma_count[q] = n + 1
        K = self.NDMASEM[q]
        semkey = (q, n % K)
        val = 16 * (n // K + 1)
        o = Op(fn, deps, False)
        o.dma = True
        o.dma_sem = semkey
        o.dma_val = val
        if n >= K:
            o.pre = (semkey, val - 16)
        self.ops[q].append(o)
        tag = ("dma", semkey, val)
        for r in reads:
            r.rd.append(tag)
        for w in writes:
            w.w = tag
            w.rd = []
        return tag

    def emit(self, nc, final_waits=()):
        with contextlib.ExitStack() as st:
            esem = {e: st.enter_context(nc.semaphore("s_" + e)) for e in self.ENGS}
            dsem = {}
            for q, K in self.NDMASEM.items():
                for i in range(K):
                    dsem[(q, i)] = st.enter_context(nc.semaphore("d_%s_%d" % (q, i)))
            sigcount = {}
            for e in self.ENGS:
                ops = self.ops[e]
                cnt = 0
                after = [0] * len(ops)
                for i, o in enumerate(ops):
                    if o.signal and not o.dma:
                        cnt += 1
                    after[i] = cnt
                need = [None] * len(ops)
                nxt = None
                for i in range(len(ops) - 1, -1, -1):
                    if ops[i].signal and not ops[i].dma:
                        nxt = after[i]
                    need[i] = nxt
                sigcount[e] = need
            final = []
            for r in final_waits:
                if r.w is not None:
                    final.append(r.w)

            def resolve(d):
                if d[0] == "eng":
                    v = sigcount[d[1]][d[2]]
                    assert v is not None, "dependency on unsignalled tail op %s" % (d,)
                    return (("e", d[1]), v)
                return (("d", d[1]), d[2])

            def semof(k):
                return esem[k[1]] if k[0] == "e" else dsem[k[1]]

            def run(e, engine):
                waited = {}
                for o in self.ops[e]:
                    ws = [resolve(d) for d in o.deps]
                    if o.pre is not None:
                        ws.append((("d", o.pre[0]), o.pre[1]))
                    for k, v in ws:
                        if waited.get(k, 0) >= v:
                            continue
                        waited[k] = v
                        engine.wait_ge(semof(k), v)
                    ins = o.fn(engine)
                    if o.dma:
                        ins.then_inc(dsem[o.dma_sem], 16)
                    elif o.signal:
                        ins.then_inc(esem[e], 1)
                if e == "sp":
                    for d in final:
                        k, v = resolve(d)
                        if waited.get(k, 0) >= v:
                            continue
                        waited[k] = v
                        engine.wait_ge(semof(k), v)

            with nc.Block() as block:
                @block.tensor
                def _(eng):
                    run("pe", eng)

                @block.scalar
                def _(eng):
                    run("act", eng)

                @block.vector
                def _(eng):
                    run("dve", eng)

                @block.gpsimd
                def _(eng):
                    run("pool", eng)

                @block.sync
                def _(eng):
                    run("sp", eng)


class _Stop(Exception):
    pass


def build_program(depth=2, stop=None):
    nc = bass.Bass("TRN2", target_bir_lowering=False)

    def din(name, shape):
        return nc.dram_tensor(name, list(shape), F32, kind="ExternalInput").ap()

    def dout(name, shape):
        return nc.dram_tensor(name, list(shape), F32, kind="ExternalOutput").ap()

    x_d = din("x", [NT, D])
    cond_d = din("cond", [128, 16])
    ck_d = din("ck", [512, 512])
    cv_d = din("cv", [512, 512])
    h0_d = din("h0", [128, 32])
    flags_d = din("flags", [128, 8])
    maskb_d = din("maskb", [128, 48])
    cos_d = din("cosT", [128, NT])
    sin_d = din("sinT", [128, NT])
    ident_d = din("ident", [128, 128])
    rperm_d = din("rperm", [128, 128])
    pp_d = din("pp", [128, NPP])
    wmod_d = din("w_mod", [2, D, 3 * D])
    wia_d = din("w_in_attn", [D, 5120])
    woa_d = din("w_out_attn", [D, D])
    wil_d = din("w_in_lru", [D, 4096])
    wol_d = din("w_out_lru", [D, D])
    wra_d = din("w_rg_a", [2, 8, 256, 256])
    wrx_d = din("w_rg_x", [2, 8, 256, 256])

    y_o = dout("y_out", [NT, D])
    k_o = dout("k_out", [NT, 512])
    v_o = dout("v_out", [NT, 512])
    st_o = dout("st_out", [128, 128])

    P = Plan()
    finals = []

    with contextlib.ExitStack() as st:
        try:
            def chk(ph):
                if stop == ph:
                    raise _Stop()

            def sb(name, shape, dt_=F32):
                return st.enter_context(nc.sbuf_tensor("sb_" + name, list(shape), dt_))

            A1 = sb("A1", [128, 16384])
            A2 = sb("A2", [128, 8192])
            A3 = sb("A3", [128, 8192])
            A4 = sb("A4", [128, 8192])
            NRING = 4
            RING = [sb("ring%d" % i, [128, 4096], BF16) for i in range(NRING)]
            ring_res = [Res("ring%d" % i) for i in range(NRING)]
            ring_n = [0]
            banks = [st.enter_context(nc.psum_tensor("bank%d" % i, [128, 512], F32)) for i in range(8)]
            bres = [Res("bank%d" % i) for i in range(8)]

            identf = sb("identf", [128, 128])
            identb = sb("identb", [128, 128], BF16)
            rpermb = sb("rpermb", [128, 128], BF16)
            onesb = sb("onesb", [128, 128], BF16)
            one11 = sb("one11", [1, 2])
            pp = sb("pp", [128, NPP])
            cond = sb("cond", [128, 16])
            condt = sb("condt", [128, 16])
            scond = sb("scond", [128, 16], BF16)
            scondf = sb("scondf", [128, 16])
            onesf = sb("onesf", [128, 2])
            h0t = sb("h0t", [128, 32])
            flags = sb("flags", [128, 8])
            maskb = sb("maskb", [128, 48])
            modT = [sb("modT%d" % l, [128, 48]) for l in range(2)]
            gmod = [sb("gmod%d" % l, [128, 16]) for l in range(2)]
            ggv = [sb("ggv%d" % l, [128, 16]) for l in range(2)]
            gq = sb("gq", [128, 2])
            wneg = sb("wneg", [128, 64])
            cl = sb("cl", [128, 32])
            hcl = sb("hcl", [128, 32])
            hba = sb("hba", [128, 32])
            hbx = sb("hbx", [128, 32])
            fin = sb("fin", [128, 128])
            fino = sb("fino", [128, 128])

            r_const = Res("const")
            r_pp = Res("pp")
            r_small = Res("small")
            r_scond = Res("scond")
            r_mod = [Res("mod0"), Res("mod1")]
            r_modsc = [Res("modsc0"), Res("modsc1")]
            r_modg = [Res("modg0"), Res("modg1")]
            r_der = Res("derived")
            r_fin = Res("fin")

            def ppc(name, j=0, w=1):
                o, _ = PP[name]
                return pp[:, o + j:o + j + w]

            def mm(out, lhsT, rhs, start, stop, reads, writes, signal=False):
                P.op("pe", lambda e: e.matmul(out, lhsT=lhsT, rhs=rhs, start=start, stop=stop),
                     reads=reads, writes=writes, signal=signal)

            def tr(out, in_, ident, reads, writes, signal=False):
                P.op("pe", lambda e: e.transpose(out=out, in_=in_, identity=ident), reads=reads, writes=writes,
                     signal=signal)

            def act(out, in_, func, reads, writes, bias=None, scale=None):
                kw = {}
                if bias is not None:
                    kw["bias"] = bias
                if scale is not None:
                    kw["scale"] = scale
                P.op("act", lambda e: e.activation(out=out, in_=in_, func=func, **kw), reads=reads, writes=writes)

            def tt(eng, out, in0, in1, op, reads, writes):
                P.op(eng, lambda e: e.tensor_tensor(out=out, in0=in0, in1=in1, op=op), reads=reads, writes=writes)

            def ts(eng, out, in0, s1, s2, op0, op1, reads, writes):
                if s2 is None:
                    P.op(eng, lambda e: e.tensor_scalar(out=out, in0=in0, scalar1=s1, scalar2=None, op0=op0),
                         reads=reads, writes=writes)
                else:
                    P.op(eng, lambda e: e.tensor_scalar(out=out, in0=in0, scalar1=s1, scalar2=s2, op0=op0, op1=op1),
                         reads=reads, writes=writes)

            def stt(out, in0, scalar, in1, op0, op1, reads, writes):
                P.op("dve", lambda e: e.scalar_tensor_tensor(out=out, in0=in0, scalar=scalar, in1=in1, op0=op0, op1=op1),
                     reads=reads, writes=writes)

            def cp(eng, out, in_, reads, writes):
                if eng == "act":
                    act(out, in_, AF.Copy, reads, writes)
                else:
                    P.op(eng, lambda e: e.tensor_copy(out=out, in_=in_), reads=reads, writes=writes)

            def recip(out, in_, reads, writes):
                P.op("dve", lambda e: e.reciprocal(out=out, in_=in_), reads=reads, writes=writes)

            def wload(dst_view_fn, src):
                s = ring_n[0] % NRING
                ring_n[0] += 1
                P.dma("pool", lambda e: e.dma_start(out=dst_view_fn(RING[s]), in_=src), writes=[ring_res[s]])
                return s

            def v3(ap2, c):
                return ap2.rearrange("p (c t) -> p c t", c=c)

            P.dma("sp", lambda e: e.dma_start(out=identf[:], in_=ident_d), writes=[r_const])
            P.dma("sp", lambda e: e.dma_start(out=pp[:], in_=pp_d), writes=[r_pp])
            r_cond, r_h0, r_flags, r_maskb = Res(), Res(), Res(), Res()
            P.dma("sp", lambda e: e.dma_start(out=cond[:], in_=cond_d), writes=[r_cond])
            P.dma("sp", lambda e: e.dma_start(out=h0t[:], in_=h0_d), writes=[r_h0])
            P.dma("sp", lambda e: e.dma_start(out=flags[:], in_=flags_d), writes=[r_flags])
            P.dma("sp", lambda e: e.dma_start(out=maskb[:], in_=maskb_d), writes=[r_maskb])
            r_cb = Res("constb")
            r_rp = Res("rpermb")
            P.dma("pool", lambda e: e.dma_start(out=identb[:], in_=ident_d), writes=[r_cb])
            P.dma("pool", lambda e: e.dma_start(out=rpermb[:], in_=rperm_d), writes=[r_rp])
            r_ones = Res("ones")
            P.op("dve", lambda e: e.memset(onesb[:], 1.0), writes=[r_ones])
            P.op("dve", lambda e: e.memset(one11[:], 1.0), writes=[r_ones])
            act(condt[:], cond[:], AF.Tanh, [r_cond], [r_scond], scale=0.5)
            stt(condt[:], condt[:], 1.0, cond[:], ALU.add, ALU.mult, [r_scond, r_cond], [r_scond])
            ts("dve", scond[:], condt[:], 0.5, None, ALU.mult, None, [r_scond], [r_scond])
            ts("dve", scondf[:], condt[:], 0.5, None, ALU.mult, None, [r_scond], [r_scond])
            P.op("dve", lambda e: e.memset(onesf[:], 1.0), writes=[r_ones])
            ts("dve", gq[:, 0:1], ppc("qn"), float(128.0 ** -0.5), None, ALU.mult, None, [r_pp], [r_der])
            cp("dve", gq[:, 1:2], ppc("kn"), [r_pp], [r_der])
            ts("dve", wneg[:], ppc("convw", 0, 64), flags[:, 1:2], None, ALU.mult, None, [r_pp, r_flags], [r_der])
            ts("dve", hba[:], ppc("ba", 0, 32), 0.5, None, ALU.mult, None, [r_pp], [r_der])
            ts("dve", hbx[:], ppc("bx", 0, 32), 0.5, None, ALU.mult, None, [r_pp], [r_der])
            act(cl[:], ppc("lam", 0, 32), AF.Exp, [r_pp], [r_der], scale=-1.0)
            act(cl[:], cl[:], AF.Ln, [r_der], [r_der], bias=1.0)
            ts("dve", hcl[:], cl[:], -4.0, None, ALU.mult, None, [r_der], [r_der])
            ts("dve", cl[:], cl[:], -8.0, None, ALU.mult, None, [r_der], [r_der])

            bg_acc = {"t": [A4[:, 6144 + i * 512:6144 + (i + 1) * 512] for i in range(2)],
                      "r": [Res(), Res()]}
            bg_pending = []

            def bg_load(l, grp):
                slots = []
                for kh in range(2):
                    src = wmod_d[l, kh * 1024:(kh + 1) * 1024, grp * 512:(grp + 1) * 512].rearrange(
                        "(k p) n -> p k n", p=128)
                    slots.append(wload(lambda r: v3(r[:], 8), src))
                return slots

            def adaln_group_dve(l, grp, tbank, acc=None, acc_res=None, slots=None, defer=None):
                if acc is None:
                    ai = grp % 2
                    acc, acc_res = bg_acc["t"][ai], bg_acc["r"][ai]
                for kh in range(2):
                    if slots is None:
                        src = wmod_d[l, kh * 1024:(kh + 1) * 1024, grp * 512:(grp + 1) * 512].rearrange(
                            "(k p) n -> p k n", p=128)
                        s_ = wload(lambda r: v3(r[:], 8), src)
                    else:
                        s_ = slots[kh]
                    sv_ = v3(RING[s_][:], 8)
                    for k in range(8):
                        kk = kh * 8 + k
                        if kk == 0:
                            ts("dve", acc, sv_[:, k, :], scondf[:, 0:1], None, ALU.mult, None,
                               [ring_res[s_], r_scond], [acc_res])
                        else:
                            stt(acc, sv_[:, k, :], scondf[:, kk:kk + 1], acc, ALU.mult, ALU.add,
                                [ring_res[s_], r_scond, acc_res], [acc_res])
                        yield

                def finalize():
                    for j in range(4):
                        mm(banks[tbank][:, j:j + 1], acc[:, j * 128:(j + 1) * 128], onesf[:, 0:1], True, True,
                           [acc_res, r_ones], [bres[tbank]], signal=(j == 3))
                    o = PP["bmod%d" % l][0]
                    tt("dve", modT[l][:, 4 * grp:4 * grp + 4], banks[tbank][:, 0:4],
                       pp[:, o + 4 * grp:o + 4 * grp + 4], ALU.add, [bres[tbank], r_pp], [r_mod[l]])
                    if (l, grp) == (0, 11):
                        adaln_finish_g(0)
                    if (l, grp) == (1, 7):
                        adaln_finish_sc(1)
                    if (l, grp) == (1, 11):
                        adaln_finish_g(1)
                (defer if defer is not None else bg_pending).append(finalize)
                yield

            def adaln_finish_sc(l):
                o = PP["gpre%d" % l][0]
                stt(gmod[l][:], modT[l][:, 16:32], 1.0, pp[:, o:o + 16], ALU.add, ALU.mult, [r_mod[l], r_pp], [r_modsc[l]])

            def adaln_finish_g(l):
                o = PP["gpost%d" % l][0]
                tt("dve", ggv[l][:], modT[l][:, 32:48], pp[:, o:o + 16], ALU.mult, [r_mod[l], r_pp], [r_modg[l]])

            bg_jobs = [(0, g) for g in range(8, 12)] + ([(1, g) for g in range(12)] if depth > 1 else [])

            bg_cur = [None, 0]

            def run_bg_half(tbank):
                while bg_pending:
                    bg_pending.pop(0)()
                if bg_cur[0] is None:
                    if not bg_jobs:
                        return
                    l_, g_ = bg_jobs.pop(0)
                    bg_cur[0] = adaln_group_dve(l_, g_, tbank)
                    bg_cur[1] = 0
                for _ in range(8):
                    next(bg_cur[0])
                bg_cur[1] += 1
                if bg_cur[1] == 2:
                    for _ in bg_cur[0]:
                        pass
                    bg_cur[0] = None

            def run_bg(rowbank, tbank):
                while bg_pending:
                    bg_pending.pop(0)()
                if not bg_jobs:
                    return
                l_, g_ = bg_jobs.pop(0)
                for _ in adaln_group_dve(l_, g_, tbank):
                    pass

            chk(0)
            def load_xT(stage, stage_res, consume):
                for tb in range(8):
                    si = tb % 2
                    P.dma("sp", lambda e, tb=tb, si=si: e.dma_start(out=stage[si], in_=x_d[tb * 128:(tb + 1) * 128, :]),
                          writes=[stage_res[si]])
                    for q in range(4):
                        for j in range(4):
                            c = 4 * q + j
                            tr(banks[q][:, j * 128:(j + 1) * 128], stage[si][:, c * 128:(c + 1) * 128], identf[:],
                               [stage_res[si], r_const], [bres[q]], signal=(j == 3))
                        consume(tb, q)

            def prenorm(yT, y_res, l, hT, h_res, sqt, sq_res, rstd, rstd_res, tmpt, tmp_res):
                for half in range(2):
                    hs = slice(half * 512, (half + 1) * 512)
                    sbk = 4 + half
                    for c in range(16):
                        qi = c % 3
                        act(sqt[qi], yT[:, c, hs], AF.Square, [y_res[c][half]], [sq_res[qi]])
                        mm(banks[sbk][:], onesb[:], sqt[qi], c == 0, c == 15, [sq_res[qi], r_ones], [bres[sbk]],
                           signal=True)
                    act(rstd[half], banks[sbk][:], AF.Ln, [bres[sbk]], [rstd_res[half]], bias=EPS, scale=1.0 / D)
                    act(rstd[half], rstd[half], AF.Exp, [rstd_res[half]], [rstd_res[half]], scale=-0.5)
                    for c in range(16):
                        ti = c % 2
                        tt("dve", tmpt[ti], yT[:, c, hs], rstd[half], ALU.mult, [y_res[c][half], rstd_res[half]],
                           [tmp_res[ti]])
                        act(hT[:, c, hs], tmpt[ti], AF.Identity, [tmp_res[ti], r_modsc[l], r_mod[l]], [h_res[c][half]],
                            bias=modT[l][:, c:c + 1], scale=gmod[l][:, c:c + 1])

            def oproj_load(w_d, sidx):
                src = w_d[:, sidx * 256:(sidx + 1) * 256].rearrange("(k p) n -> p k n", p=128)
                return wload(lambda r: v3(r[:], 16), src)

            def outproj(w_d, ogT, og_res, mT, m_res, sqt, sq_res, rstd, rstd_res, before_slot=None, pre_slots=None):
                for sidx in range(8):
                    if pre_slots and sidx in pre_slots:
                        s = pre_slots[sidx]
                    else:
                        s = oproj_load(w_d, sidx)
                    if before_slot is not None:
                        before_slot(sidx)
                    sv = v3(RING[s][:], 16)
                    for mcl in range(2):
                        mc = 2 * sidx + mcl
                        for half in range(2):
                            hs = slice(half * 512, (half + 1) * 512)
                            bk = (2 * mc + half) % 4
                            for k in range(16):
                                mm(banks[bk][:], sv[:, k, mcl * 128:(mcl + 1) * 128], ogT[:, k, hs], k == 0, k == 15,
                                   [ring_res[s], og_res[k][half]], [bres[bk]], signal=(k == 15))
                            qi = (2 * mc + half) % 3
                            act(sqt[qi], banks[bk][:], AF.Square, [bres[bk]], [sq_res[qi], bres[bk]])
                            cp("dve", mT[:, mc, hs], banks[bk][:], [bres[bk]], [m_res[mc][half]])
                            mm(banks[4 + half][:], onesb[:], sqt[qi], mc == 0, mc == 15, [sq_res[qi], r_ones],
                               [bres[4 + half]], signal=True)
                for half in range(2):
                    act(rstd[half], banks[4 + half][:], AF.Ln, [bres[4 + half]], [rstd_res[half]], bias=EPS,
                        scale=1.0 / D)
                    act(rstd[half], rstd[half], AF.Exp, [rstd_res[half]], [rstd_res[half]], scale=-0.5)

            xT = v3(A1[:], 16)
            xT_res = [[Res() for _ in range(2)] for _ in range(16)]
            xs = [A3[:, 0:2048], A3[:, 2048:4096]]
            xs_res = [Res(), Res()]
            def consume_x(tb, q):
                half = tb // 4
                dst = xT[:, 4 * q:4 * q + 4, tb * 128:(tb + 1) * 128]
                src = v3(banks[q][:], 4)
                eng = "act" if (q % 2 == 0) else "dve"
                cp(eng, dst, src, [bres[q]], [xT_res[4 * q + j][half] for j in range(4)])
            load_xT(xs, xs_res, consume_x)
            for grp in range(8):
                while bg_pending:
                    bg_pending.pop(0)()
                for _ in adaln_group_dve(0, grp, 6):
                    pass
            while bg_pending:
                bg_pending.pop(0)()
            adaln_finish_sc(0)

            hT = v3(A2[:].bitcast(BF16), 16)
            hT_res = [[Res() for _ in range(2)] for _ in range(16)]
            a4b = A4[:].bitcast(BF16)
            sqt = [a4b[:, i * 512:(i + 1) * 512] for i in range(3)]
            sq_res = [Res() for _ in range(3)]
            rstd = [A4[:, 1024 + i * 512:1024 + (i + 1) * 512] for i in range(2)]
            rstd_res = [Res(), Res()]
            tmpt = [A4[:, 2048 + i * 512:2048 + (i + 1) * 512] for i in range(2)]
            tmp_res = [Res(), Res()]
            prenorm(xT, xT_res, 0, hT, hT_res, sqt, sq_res, rstd, rstd_res, tmpt, tmp_res)

            chk(1)
            dead = retire([r for row in xT_res for r in row])
            gT = v3(A1[:, 0:8192].bitcast(BF16), 16)
            gT_res = [[Res(deps=dead) for _ in range(2)] for _ in range(16)]
            cosT = A1[:, 8192:9216]
            sinT = A1[:, 9216:10240]
            r_rope = Res("rope", deps=dead)
            r_rope2 = Res("rope2", deps=dead)
            P.dma("sp", lambda e: e.dma_start(out=cosT, in_=cos_d), writes=[r_rope])
            P.dma("sp", lambda e: e.dma_start(out=sinT, in_=sin_d), writes=[r_rope2])
            a1b = A1[:, 10240:16384]
            a1bb = a1b.bitcast(BF16)
            sq2 = [a1bb[:, i * 512:(i + 1) * 512] for i in range(3)]
            sq2_res = [Res(deps=dead) for _ in range(3)]
            qn2 = [a1bb[:, 1536 + i * 512:1536 + (i + 1) * 512] for i in range(3)]
            qn2_res = [Res(deps=dead) for _ in range(3)]
            rs2 = [a1b[:, 1536 + i * 512:1536 + (i + 1) * 512] for i in range(2)]
            rs2_res = [Res(deps=dead) for _ in range(2)]
            t1 = [a1b[:, 2560 + i * 512:2560 + (i + 1) * 512] for i in range(2)]
            t1_res = [Res(deps=dead) for _ in range(2)]
            t2 = [a1b[:, 3584 + i * 512:3584 + (i + 1) * 512] for i in range(2)]
            t2_res = [Res(deps=dead) for _ in range(2)]
            tg = [a1b[:, 4608 + i * 512:4608 + (i + 1) * 512] for i in range(2)]
            tg_res = [Res(deps=dead) for _ in range(2)]
            vst = [a1b[:, 5632 + i * 256:5632 + (i + 1) * 256] for i in range(2)]
            vst_res = [Res(deps=dead) for _ in range(2)]

            dead4 = retire(sq_res + rstd_res + tmp_res)
            kT = v3(A4[:, 0:3072].bitcast(BF16), 4)
            kT_res = [[Res(deps=dead4) for _ in range(12)] for _ in range(4)]
            Vb = v3(A4[:, 3072:6144].bitcast(BF16), 12)
            V_res = [Res(deps=dead4) for _ in range(12)]
            ckst = v3(A4[:, 6144:8192], 4)
            ckst_res = Res(deps=dead4 + retire(bg_acc["r"]))
            dead3 = retire(xs_res)
            qT = v3(A3[:].bitcast(BF16), 16)
            qT_res = [[Res(deps=dead3) for _ in range(2)] for _ in range(16)]

            def inproj_slot(col0):
                src = wia_d[:, col0:col0 + 256].rearrange("(k p) n -> p k n", p=128)
                s = wload(lambda r: v3(r[:], 16), src)
                return s, v3(RING[s][:], 16)

            acc_n = [0]

            def acc_fm(sv, s, mcl, half, h_res_):
                bk = acc_n[0] % 4
                acc_n[0] += 1
                hs = slice(half * 512, (half + 1) * 512)
                for k in range(16):
                    mm(banks[bk][:], sv[:, k, mcl * 128:(mcl + 1) * 128], hT[:, k, hs], k == 0, k == 15,
                       [ring_res[s], h_res_[k][half]], [bres[bk]], signal=(k == 15))
                return bk

            nr_n = [0]

            def norm_rope(bk, gcol, half, dst, dst_res):
                i = nr_n[0] % 2
                j3 = nr_n[0] % 3
                nr_n[0] += 1
                hs = slice(half * 512, (half + 1) * 512)
                act(sq2[j3], banks[bk][:], AF.Square, [bres[bk]], [sq2_res[j3]])
                yield
                mm(banks[6][:], onesb[:], sq2[j3], True, True, [sq2_res[j3], r_ones], [bres[6]], signal=True)
                act(rs2[i], banks[6][:], AF.Ln, [bres[6]], [rs2_res[i]], bias=EPS, scale=1.0 / 128)
                act(rs2[i], rs2[i], AF.Exp, [rs2_res[i]], [rs2_res[i]], scale=-0.5)
                stt(qn2[j3], banks[bk][:], gq[:, gcol:gcol + 1], rs2[i], ALU.mult, ALU.mult,
                    [bres[bk], r_der, rs2_res[i]], [qn2_res[j3]])
                yield
                mm(banks[7][:], rpermb[:], qn2[j3], True, True, [qn2_res[j3], r_rp], [bres[7]], signal=True)
                tt("dve", t1[i], qn2[j3], cosT[:, hs], ALU.mult, [qn2_res[j3], r_rope], [t1_res[i]])
                tt("dve", t2[i], banks[7][:], sinT[:, hs], ALU.mult, [bres[7], r_rope2], [t2_res[i]])
                tt("dve", dst, t1[i], t2[i], ALU.add, [t1_res[i], t2_res[i]],
                   dst_res if isinstance(dst_res, list) else [dst_res])

            def run_pipeline(tiles):
                gens = []

                def fin_(g):
                    for _ in g:
                        pass
                for i_, (accf, args) in enumerate(tiles):
                    bk_ = accf()
                    g = norm_rope(bk_, *args)
                    next(g)
                    gens.append(g)
                    if i_ >= 1:
                        next(gens[i_ - 1])
                    if i_ >= 2:
                        fin_(gens[i_ - 2])
                n_ = len(tiles)
                if n_ >= 1:
                    next(gens[n_ - 1])
                if n_ >= 2:
                    fin_(gens[n_ - 2])
                if n_ >= 1:
                    fin_(gens[n_ - 1])

            P.dma("sp", lambda e: e.dma_start(out=ckst, in_=ck_d.rearrange("(kb p) n -> p kb n", p=128)),
                  writes=[ckst_res])
            for vs in range(2):
                s, sv = inproj_slot(2560 + vs * 256)
                for tb in range(8):
                    bk = 4 + (tb % 2)
                    half = tb // 4
                    for k in range(16):
                        mm(banks[bk][:, 0:256], hT[:, k, tb * 128:(tb + 1) * 128], sv[:, k, :], k == 0, k == 15,
                           [ring_res[s], hT_res[k][half]], [bres[bk]], signal=(k == 15))
                    cp("act", Vb[:, 4 + tb, vs * 256:(vs + 1) * 256], banks[bk][:, 0:256], [bres[bk]],
                       [V_res[4 + tb], bres[bk]])
                    vi = tb % 2
                    cp("dve", vst[vi], banks[bk][:, 0:256], [bres[bk]], [vst_res[vi]])
                    rr = Res()
                    P.dma("sp", lambda e, tb=tb, vs=vs, vi=vi: e.dma_start(
                        out=v_o[tb * 128:(tb + 1) * 128, vs * 256:(vs + 1) * 256], in_=vst[vi]),
                        reads=[vst_res[vi]], writes=[rr])
                    finals.append(rr)
            chk(20)
            for kb in range(4):
                bk = 4 + (kb % 2)
                for kvh in range(4):
                    tr(banks[bk][:, kvh * 128:(kvh + 1) * 128], ckst[:, kb, kvh * 128:(kvh + 1) * 128], identf[:],
                       [ckst_res, r_const], [bres[bk]], signal=(kvh == 3))
                cp("act", kT[:, :, kb * 128:(kb + 1) * 128], v3(banks[bk][:], 4), [bres[bk]],
                   [kT_res[kvh][kb] for kvh in range(4)])
            chk(21)
            deadck = retire([ckst_res])
            bg_acc["r"] = [Res(deps=deadck) for _ in range(2)]
            ktiles = []
            for ks in range(2):
                for mcl in range(2):
                    kvh = 2 * ks + mcl
                    for half in range(2):
                        def accf(ks=ks, mcl=mcl, half=half, first=(mcl == 0 and half == 0), cell=[None]):
                            if first:
                                run_bg_half(5)
                                kslot[ks] = inproj_slot(2048 + ks * 256)
                            s_, sv_ = kslot[ks]
                            return acc_fm(sv_, s_, mcl, half, hT_res)
                        ktiles.append((accf, (1, half, kT[:, kvh, 512 + half * 512:512 + (half + 1) * 512],
                                              [kT_res[kvh][4 + 4 * half + j] for j in range(4)])))
            kslot = {}
            run_pipeline(ktiles)
            P.dma("pool", lambda e: e.dma_start(out=Vb[:, 0:4, :], in_=cv_d.rearrange("(kb p) n -> p kb n", p=128)),
                  writes=V_res[0:4])
            chk(22)
            for tb in range(8):
                bk = 4 + (tb % 2)
                pb = banks[bk][:].bitcast(BF16)
                for kvh in range(4):
                    tr(pb[:, kvh * 128:(kvh + 1) * 128], kT[:, kvh, 512 + tb * 128:512 + (tb + 1) * 128], identb[:],
                       [kT_res[kvh][4 + tb], r_cb], [bres[bk]], signal=(kvh == 3))
                i = tb % 2
                cp("act", t1[i], pb[:, 0:512], [bres[bk]], [t1_res[i]])
                rr = Res()
                P.dma("sp", lambda e, tb=tb, i=i: e.dma_start(out=k_o[tb * 128:(tb + 1) * 128, :], in_=t1[i]),
                      reads=[t1_res[i]], writes=[rr])
                finals.append(rr)
            chk(23)
            for gs in range(8):
                if gs < 6:
                    run_bg_half(5)
                else:
                    while bg_pending:
                        bg_pending.pop(0)()
                s, sv = inproj_slot(3072 + gs * 256)
                for mcl in range(2):
                    c = 2 * gs + mcl
                    for half in range(2):
                        hs = slice(half * 512, (half + 1) * 512)
                        bk = acc_fm(sv, s, mcl, half, hT_res)
                        i = (2 * c + half) % 2
                        act(tg[i], banks[bk][:], AF.Tanh, [bres[bk]], [tg_res[i]], scale=0.5)
                        stt(gT[:, c, hs], tg[i], 1.0, banks[bk][:], ALU.add, ALU.mult, [tg_res[i], bres[bk]],
                            [gT_res[c][half]])
            chk(24)
            qtiles = []
            qslot = {}
            for qs in range(8):
                for mcl in range(2):
                    h = 2 * qs + mcl
                    for half in range(2):
                        def accf(qs=qs, mcl=mcl, half=half, first=(mcl == 0 and half == 0)):
                            if first:
                                qslot[qs] = inproj_slot(qs * 256)
                            s_, sv_ = qslot[qs]
                            return acc_fm(sv_, s_, mcl, half, hT_res)
                        qtiles.append((accf, (0, half, qT[:, h, half * 512:(half + 1) * 512], qT_res[h][half])))
            run_pipeline(qtiles)

            chk(2)
            dead2 = retire([r for row in hT_res for r in row])
            a2b = A2[:].bitcast(BF16)
            NPT = 8
            Pt = [a2b[:, i * 512:(i + 1) * 512] for i in range(NPT)]
            Pt_res = [Res(deps=dead2) for _ in range(NPT)]
            rden = [A2[:, 2048 + i * 512:2048 + (i + 1) * 512] for i in range(2)]
            rden_res = [Res(deps=dead2) for _ in range(2)]
            otmp = [A2[:, 3072 + i * 512:3072 + (i + 1) * 512] for i in range(2)]
            otmp_res = [Res(deps=dead2) for _ in range(2)]
            items = [(h, c, kb) for h in range(16) for c in range(2) for kb in range(12)]
            SB = [0, 1, 6, 7]

            def emit_S(i):
                h, c, kb = items[i]
                kvh = h // 4
                bk = SB[i % 4]
                mm(banks[bk][:], kT[:, kvh, kb * 128:(kb + 1) * 128], qT[:, h, c * 512:(c + 1) * 512], True, True,
                   [kT_res[kvh][kb], qT_res[h][c]], [bres[bk]], signal=True)

            def emit_exp(i):
                h, c, kb = items[i]
                bk = SB[i % 4]
                pi = i % NPT
                act(Pt[pi], banks[bk][:], AF.Exp, [bres[bk]], [Pt_res[pi]])
                col = kb * 4 + 2 * c
                diag = kb >= 4 and ((kb - 4) // 4 == c)
                if diag:
                    for hh in range(2):
                        ts("dve", Pt[pi][:, hh * 256:(hh + 1) * 256], Pt[pi][:, hh * 256:(hh + 1) * 256],
                           maskb[:, col + hh:col + hh + 1], None, ALU.mult, None, [Pt_res[pi], r_maskb], [Pt_res[pi]])
                else:
                    ts("dve", Pt[pi], Pt[pi], maskb[:, col:col + 1], None, ALU.mult, None, [Pt_res[pi], r_maskb],
                       [Pt_res[pi]])

            def emit_PV(i):
                h, c, kb = items[i]
                kvh = h // 4
                pi = i % NPT
                hc = (2 * h + c) % 2
                ob = 2 + hc
                db = 4 + hc
                mm(banks[ob][:], Vb[:, kb, kvh * 128:(kvh + 1) * 128], Pt[pi], kb == 0, kb == 11,
                   [V_res[kb], Pt_res[pi]], [bres[ob]], signal=(kb == 11))
                mm(banks[db][:], onesb[:], Pt[pi], kb == 0, kb == 11, [Pt_res[pi], r_ones], [bres[db]],
                   signal=(kb == 11))

            def emit_epi(h, c):
                hc = (2 * h + c) % 2
                ob = 2 + hc
                db = 4 + hc
                cs = slice(c * 512, (c + 1) * 512)
                act(rden[hc], banks[db][:], AF.Ln, [bres[db]], [rden_res[hc]])
                act(rden[hc], rden[hc], AF.Exp, [rden_res[hc]], [rden_res[hc]], scale=-1.0)
                tt("dve", otmp[hc], banks[ob][:], rden[hc], ALU.mult, [bres[ob], rden_res[hc]], [otmp_res[hc]])
                stt(qT[:, h, cs], otmp[hc], 0.5, gT[:, h, cs], ALU.mult, ALU.mult, [otmp_res[hc], gT_res[h][c]],
                    [qT_res[h][c]])

            LOOK = 3
            pre4 = {}
            cur_job = [None]
            att_jobs = list(bg_jobs[:4])
            del bg_jobs[:4]
            att_acc = [A2[:, 4096 + n * 512:4096 + (n + 1) * 512] for n in range(8)]
            att_acc_res = [Res(deps=dead2) for _ in range(8)]
            att_final = []
            att_slots = {}
            att_n = [0]
            assert len(att_jobs) <= 8
            if att_jobs:
                att_slots[0] = bg_load(*att_jobs[0])
            pend_epi = []
            for i0 in range(LOOK):
                emit_S(i0)
            for i in range(len(items)):
                h, c, kb = items[i]
                if i + LOOK < len(items):
                    emit_S(i + LOOK)
                emit_exp(i)
                emit_PV(i)
                if pend_epi and kb == 3:
                    emit_epi(*pend_epi.pop(0))
                if i % 6 == 5:
                    if cur_job[0] is None and att_n[0] < len(att_jobs):
                        n_ = att_n[0]
                        att_n[0] += 1
                        l_, g_ = att_jobs[n_]
                        cur_job[0] = adaln_group_dve(l_, g_, 7, acc=att_acc[n_], acc_res=att_acc_res[n_],
                                                     slots=att_slots[n_], defer=att_final)
                        if n_ + 1 < len(att_jobs):
                            att_slots[n_ + 1] = bg_load(*att_jobs[n_ + 1])
                        else:
                            pre4[0] = oproj_load(woa_d, 0)
                            pre4[1] = oproj_load(woa_d, 1)
                    if cur_job[0] is not None:
                        try:
                            next(cur_job[0])
                        except StopIteration:
                            cur_job[0] = None
                if kb == 11:
                    pend_epi.append((h, c))
            while pend_epi:
                emit_epi(*pend_epi.pop(0))
            if cur_job[0] is not None:
                for _ in cur_job[0]:
                    pass
            while att_n[0] < len(att_jobs):
                n_ = att_n[0]
                att_n[0] += 1
                l_, g_ = att_jobs[n_]
                for _ in adaln_group_dve(l_, g_, 7, acc=att_acc[n_], acc_res=att_acc_res[n_],
                                         slots=att_slots.get(n_), defer=att_final):
                    pass
            while bg_pending:
                bg_pending.pop(0)()
            for f_ in att_final:
                f_()

            chk(3)
            dead1 = retire([r for row in gT_res for r in row] + [r_rope, r_rope2] + sq2_res + qn2_res + rs2_res + t1_res +
                           t2_res + tg_res + vst_res)
            mT = v3(A1[:], 16)
            m_res = [[Res(deps=dead1) for _ in range(2)] for _ in range(16)]
            a2c = A2[:, 4096:8192]
            a2cb = a2c.bitcast(BF16)
            sq4 = [a2cb[:, i * 512:(i + 1) * 512] for i in range(3)]
            dead_att = retire(att_acc_res)
            sq4_res = [Res(deps=dead2 + dead_att) for _ in range(3)]
            rstd4 = [a2c[:, 1024 + i * 512:1024 + (i + 1) * 512] for i in range(2)]
            rstd4_res = [Res(deps=dead2 + dead_att) for _ in range(2)]
            p4_left = [4 if depth > 1 else 0]

            def p4_hook(sidx):
                if p4_left[0] > 0 or bg_cur[0] is not None:
                    if bg_cur[0] is None:
                        p4_left[0] -= 1
                    run_bg_half(7)
                else:
                    while bg_pending:
                        bg_pending.pop(0)()
            outproj(woa_d, qT, qT_res, mT, m_res, sq4, sq4_res, rstd4, rstd4_res, before_slot=p4_hook,
                    pre_slots=pre4)
            while bg_cur[0] is not None:
                run_bg_half(7)
            while bg_pending:
                bg_pending.pop(0)()
            dead4b = retire([r for row in kT_res for r in row] + V_res + [ckst_res] + bg_acc["r"])
            d0T = v3(A4[:].bitcast(BF16), 16)
            d0_res = [[Res(deps=dead4b) for _ in range(2)] for _ in range(16)]
            for half in range(2):
                hs = slice(half * 512, (half + 1) * 512)
                for c in range(16):
                    stt(d0T[:, c, hs], mT[:, c, hs], ggv[0][:, c:c + 1], rstd4[half], ALU.mult, ALU.mult,
                        [m_res[c][half], r_modg[0], rstd4_res[half]], [d0_res[c][half]])

            if depth == 1:
                deadA2 = retire(Pt_res + rden_res + otmp_res + sq4_res + rstd4_res + att_acc_res)
                xs7 = [A2[:, i * 2048:(i + 1) * 2048] for i in range(2)]
                xs7_res = [Res(deps=deadA2) for _ in range(2)]
                os7 = [A2[:, 4096 + i * 2048:4096 + (i + 1) * 2048] for i in range(2)]
                os7_res = [Res(deps=deadA2) for _ in range(2)]
                for tb in range(8):
                    si = tb % 2
                    half = tb // 4
                    P.dma("sp", lambda e, tb=tb, si=si: e.dma_start(out=xs7[si], in_=x_d[tb * 128:(tb + 1) * 128, :]),
                          writes=[xs7_res[si]])
                    for q in range(2):
                        pb = banks[q][:].bitcast(BF16)
                        for j in range(8):
                            c = 8 * q + j
                            tr(pb[:, j * 128:(j + 1) * 128], d0T[:, c, tb * 128:(tb + 1) * 128], identb[:],
                               [d0_res[c][half], r_cb], [bres[q]], signal=(j == 7))
                        tt("dve", os7[si][:, q * 1024:(q + 1) * 1024], pb[:, 0:1024], xs7[si][:, q * 1024:(q + 1) * 1024],
                           ALU.add, [bres[q], xs7_res[si]], [os7_res[si]])
                    rr = Res()
                    P.dma("sp", lambda e, tb=tb, si=si: e.dma_start(out=y_o[tb * 128:(tb + 1) * 128, :], in_=os7[si]),
                          reads=[os7_res[si]], writes=[rr])
                    finals.append(rr)
                rr = Res()
                P.op("dve", lambda e: e.memset(fino[:], 0.0), writes=[r_fin])
                P.dma("sp", lambda e: e.dma_start(out=st_o, in_=fino[:]), reads=[r_fin], writes=[rr])
                finals.append(rr)
                raise _Stop()

            gwt = [sb("gwt%d" % i, [128, 2048], BF16) for i in range(2)]
            gwt_res = [[Res() for _ in range(4)] for _ in range(2)]
            xslot = {}

            def load_slot(j, which):
                col0 = (0 if which == 0 else 2048) + j * 256
                src = wil_d[:, col0:col0 + 256].rearrange("(k p) n -> p k n", p=128)
                s_ = wload(lambda r: v3(r[:], 16), src)
                xslot[(j, which)] = (s_, v3(RING[s_][:], 16))

            def load_gw(j):
                gi = j % 2
                gvj = gwt[gi][:].rearrange("p (m k n) -> p m k n", m=4, k=2)
                for m in range(4):
                    d__ = m // 2
                    wsrc = (wra_d if m % 2 == 0 else wrx_d)[d__, j].rearrange("(k p) n -> p k n", p=128)
                    P.dma("pool", lambda e, gvj=gvj, m=m, wsrc=wsrc: e.dma_start(out=gvj[:, m], in_=wsrc),
                          writes=[gwt_res[gi][m]])

            load_slot(0, 0)
            load_gw(0)
            load_slot(0, 1)
            chk(4)
            deadm = retire([r for row in m_res for r in row])
            y1T = v3(A1[:], 16)
            y1_res = [[Res(deps=deadm) for _ in range(2)] for _ in range(16)]
            dead3b = retire([r for row in qT_res for r in row])
            xs5 = [A3[:, 0:2048], A3[:, 2048:4096]]
            xs5_res = [Res(deps=dead3b), Res(deps=dead3b)]
            sq5 = [A3[:, 4096:8192].bitcast(BF16)[:, i * 512:(i + 1) * 512] for i in range(3)]
            sq5_res = [Res(deps=dead3b) for _ in range(3)]
            rstd5 = [A3[:, 5120 + i * 512:5120 + (i + 1) * 512] for i in range(2)]
            rstd5_res = [Res(deps=dead3b) for _ in range(2)]
            tmp5 = [A3[:, 6144 + i * 512:6144 + (i + 1) * 512] for i in range(2)]
            tmp5_res = [Res(deps=dead3b) for _ in range(2)]

            def consume_x5(tb, q):
                half = tb // 4
                bs = slice(tb * 128, (tb + 1) * 128)
                tt("dve", y1T[:, 4 * q:4 * q + 4, bs], v3(banks[q][:], 4), d0T[:, 4 * q:4 * q + 4, bs], ALU.add,
                   [bres[q]] + [d0_res[4 * q + j][half] for j in range(4)], [y1_res[4 * q + j][half] for j in range(4)])
            load_xT(xs5, xs5_res, consume_x5)
            deadA2 = retire(Pt_res + rden_res + otmp_res + sq4_res + rstd4_res + att_acc_res)
            h1T = v3(A2[:].bitcast(BF16), 16)
            h1_res = [[Res(deps=deadA2) for _ in range(2)] for _ in range(16)]
            prenorm(y1T, y1_res, 1, h1T, h1_res, sq5, sq5_res, rstd5, rstd5_res, tmp5, tmp5_res)

            chk(5)
            deady = retire([r for row in y1_res for r in row])
            dead3c = retire(xs5_res + sq5_res + rstd5_res + tmp5_res)
            ogT = v3(A3[:].bitcast(BF16), 16)
            og_res = [[Res(deps=dead3c) for _ in range(2)] for _ in range(16)]
            XW = 1028
            xbp = [A1[:, i * XW:(i + 1) * XW] for i in range(2)]
            xbp_res = [Res(deps=deady) for _ in range(2)]
            xc = [A1[:, 2064 + i * 1024:2064 + (i + 1) * 1024] for i in range(2)]
            xc_res = [Res(deps=deady) for _ in range(2)]
            xcb = [A1[:, 4112 + i * 512:4112 + (i + 1) * 512].bitcast(BF16) for i in range(2)]
            xcb_res = [Res(deps=deady) for _ in range(2)]
            g2 = [A1[:, 5136 + i * 512:5136 + (i + 1) * 512].bitcast(BF16) for i in range(2)]
            g2_res = [Res(deps=deady) for _ in range(2)]
            hf = A1[:, 6160:7184]
            hb = A1[:, 7184:8208]
            hf_res = Res(deps=deady)
            hb_res = Res(deps=deady)
            trt = [A1[:, 8208 + i * 512:8208 + (i + 1) * 512] for i in range(3)]
            trt_res = [Res(deps=deady) for _ in range(3)]
            tit = [A1[:, 9744 + i * 256:9744 + (i + 1) * 256].bitcast(BF16) for i in range(4)]
            tit_res = [Res(deps=deady) for _ in range(4)]
            at = [A1[:, 10768 + i * 1024:10768 + (i + 1) * 1024] for i in range(2)]
            at_res = [Res(deps=deady) for _ in range(2)]
            sqt6 = [A1[:, 12816 + i * 512:12816 + (i + 1) * 512] for i in range(4)]
            sqt6_res = [Res(deps=deady) for _ in range(4)]
            P.op("dve", lambda e: e.memset(fin[:], 0.0), writes=[r_fin])

            cnt6 = [0]
            g2b = [g2, [A1[:, 14864 + i * 512:14864 + (i + 1) * 512].bitcast(BF16) for i in range(2)]]
            g2b_res = [g2_res, [Res(deps=deady) for _ in range(2)]]

            def emit_proj_part(j, part):
                mcl = part % 2
                c = 2 * j + mcl
                s_, sv = xslot[(j, part // 2)]
                for half in range(2):
                    hs = slice(half * 512, (half + 1) * 512)
                    bk = acc_n[0] % 4
                    acc_n[0] += 1
                    for k in range(16):
                        mm(banks[bk][:], sv[:, k, mcl * 128:(mcl + 1) * 128], h1T[:, k, hs], k == 0, k == 15,
                           [ring_res[s_], h1_res[k][half]], [bres[bk]], signal=(k == 15))
                    if part < 2:
                        cp("act", xbp[mcl][:, 1 + half * 512:1 + (half + 1) * 512], banks[bk][:], [bres[bk]],
                           [xbp_res[mcl]])
                    else:
                        i = cnt6[0] % 3
                        cnt6[0] += 1
                        act(trt[i], banks[bk][:], AF.Tanh, [bres[bk]], [trt_res[i]], scale=0.5)
                        stt(g2b[j % 2][mcl][:, hs], trt[i], 1.0, banks[bk][:], ALU.add, ALU.mult,
                            [trt_res[i], bres[bk]], [g2b_res[j % 2][mcl]])

            def emit_conv(j, mcl):
                ow = PP["convw"][0]
                ob_ = PP["convb"][0]
                if True:
                    c = 2 * j + mcl
                    ts("dve", xc[mcl], xbp[mcl][:, 0:1024], pp[:, ow + c:ow + c + 1], pp[:, ob_ + c:ob_ + c + 1],
                       ALU.mult, ALU.add, [xbp_res[mcl], r_pp], [xc_res[mcl]])
                    for jj in range(1, 4):
                        stt(xc[mcl], xbp[mcl][:, jj:jj + 1024], pp[:, ow + jj * 16 + c:ow + jj * 16 + c + 1],
                            xc[mcl], ALU.mult, ALU.add, [xbp_res[mcl], r_pp, xc_res[mcl]], [xc_res[mcl]])

                    def fix(dst_t0, src_t0, jj, mcl=mcl, c=c):
                        dv = xc[mcl][:, dst_t0:dst_t0 + 513:256]
                        sv_ = xbp[mcl][:, src_t0 + 1:src_t0 + 1 + 513:256]
                        stt(dv, sv_, wneg[:, jj * 16 + c:jj * 16 + c + 1], dv, ALU.mult, ALU.add,
                            [xbp_res[mcl], r_der, xc_res[mcl]], [xc_res[mcl]])
                    fix(256, 255, 0)
                    fix(255, 256, 2)
                    fix(255, 257, 3)
                    fix(254, 256, 3)

            def emit_xcb(j):
                for mcl in range(2):
                    cp("dve", xcb[mcl], xc[mcl], [xc_res[mcl]], [xcb_res[mcl]])

            def emit_group(j, g, mid=None):
                mcl = g // 2
                d_ = g % 2
                c = 2 * j + mcl
                gi = j % 2
                gv = gwt[gi][:].rearrange("p (m k n) -> p m k n", m=4, k=2)
                dc = d_ * 16 + c
                ai = d_
                ks = []
                for half in range(2):
                    hs = slice(half * 512, (half + 1) * 512)
                    bkr = 4 + (cnt6[0] % 2)
                    bki = 6 + (cnt6[0] % 2)
                    i3 = cnt6[0] % 3
                    i4 = cnt6[0] % 4
                    cnt6[0] += 1
                    ks.append((half, hs, i3, i4))
                    for kc in range(2):
                        mm(banks[bkr][:], gv[:, 2 * d_, kc, mcl * 128:(mcl + 1) * 128], xcb[kc][:, hs],
                           kc == 0, kc == 1, [gwt_res[gi][2 * d_], xcb_res[kc]], [bres[bkr]], signal=(kc == 1))
                    for kc in range(2):
                        mm(banks[bki][:], gv[:, 2 * d_ + 1, kc, mcl * 128:(mcl + 1) * 128], xcb[kc][:, hs],
                           kc == 0, kc == 1, [gwt_res[gi][2 * d_ + 1], xcb_res[kc]], [bres[bki]], signal=(kc == 1))
                    act(trt[i3], banks[bkr][:], AF.Tanh, [bres[bkr], r_der], [trt_res[i3]],
                        bias=hba[:, dc:dc + 1], scale=0.5)
                    act(tit[i4], banks[bki][:], AF.Tanh, [bres[bki], r_der], [tit_res[i4]],
                        bias=hbx[:, dc:dc + 1], scale=0.5)
                for (half, hs, i3, i4) in ks:
                    stt(tit[i4], tit[i4], 1.0, xc[mcl][:, hs], ALU.add, ALU.mult, [tit_res[i4], xc_res[mcl]],
                        [tit_res[i4]])
                if mid is not None:
                    mid()
                for (half, hs, i3, i4) in ks:
                    act(at[ai][:, hs], trt[i3], AF.Exp, [trt_res[i3], r_der], [at_res[ai]],
                        bias=hcl[:, dc:dc + 1], scale=hcl[:, dc:dc + 1])
                    act(sqt6[i4], trt[i3], AF.Exp, [trt_res[i3], r_der], [sqt6_res[i4]],
                        bias=cl[:, dc:dc + 1], scale=cl[:, dc:dc + 1])
                for (half, hs, i3, i4) in ks:
                    act(sqt6[i4], sqt6[i4], AF.Sqrt, [sqt6_res[i4]], [sqt6_res[i4]], bias=0.25, scale=-0.25)
                for (half, hs, i3, i4) in ks:
                    tt("dve", sqt6[i4], sqt6[i4], tit[i4], ALU.mult, [sqt6_res[i4], tit_res[i4]], [sqt6_res[i4]])
                if d_ == 0:
                    av = at[ai][:, 256:769:256]
                else:
                    av = at[ai][:, 255:768:256]
                ts("dve", av, av, flags[:, 0:1], None, ALU.mult, None, [at_res[ai], r_flags], [at_res[ai]])
                (_, hs0, _, k0), (_, hs1, _, k1) = ks
                fv = fin[:].rearrange("p (s r) -> p s r", s=4)[:, :, dc]
                if d_ == 0:
                    P.op("dve", lambda e: e.tensor_tensor_scan(
                        out=hf[:, 0:512], data0=at[ai][:, 0:512], data1=sqt6[k0], initial=h0t[:, dc:dc + 1],
                        op0=ALU.mult, op1=ALU.add), reads=[at_res[ai], sqt6_res[k0], r_h0], writes=[hf_res])
                    P.op("dve", lambda e: e.tensor_tensor_scan(
                        out=hf[:, 512:1024], data0=at[ai][:, 512:1024], data1=sqt6[k1], initial=hf[:, 511:512],
                        op0=ALU.mult, op1=ALU.add), reads=[at_res[ai], sqt6_res[k1], hf_res], writes=[hf_res])
                    cp("dve", fv, hf[:, 255:1024:256], [hf_res], [r_fin])
                else:
                    P.op("dve", lambda e: e.tensor_tensor_scan(
                        out=hb[:, 512:1024][:, ::-1], data0=at[ai][:, 512:1024][:, ::-1], data1=sqt6[k1][:, ::-1],
                        initial=h0t[:, dc:dc + 1], op0=ALU.mult, op1=ALU.add),
                        reads=[at_res[ai], sqt6_res[k1], r_h0], writes=[hb_res])
                    P.op("dve", lambda e: e.tensor_tensor_scan(
                        out=hb[:, 0:512][:, ::-1], data0=at[ai][:, 0:512][:, ::-1], data1=sqt6[k0][:, ::-1],
                        initial=hb[:, 512:513], op0=ALU.mult, op1=ALU.add),
                        reads=[at_res[ai], sqt6_res[k0], hb_res], writes=[hb_res])
                    cp("dve", fv, hb[:, 0:1024:256], [hb_res], [r_fin])
                    tt("dve", hf, hf, hb, ALU.add, [hf_res, hb_res], [hf_res])
                    stt(ogT[:, c, :], hf, 0.5, g2b[j % 2][mcl], ALU.mult, ALU.mult, [hf_res, g2b_res[j % 2][mcl]],
                        [og_res[c][0], og_res[c][1]])

            for mcl_ in range(2):
                P.op("dve", lambda e, mcl_=mcl_: e.memset(xbp[mcl_][:, 0:1], 0.0), writes=[xbp_res[mcl_]])
                P.op("dve", lambda e, mcl_=mcl_: e.memset(xbp[mcl_][:, 1025:1028], 0.0), writes=[xbp_res[mcl_]])
            for j in range(8):
                if j + 1 < 8:
                    load_slot(j + 1, 0)
                emit_proj_part(j, 0)
                if j >= 1:
                    emit_group(j - 1, 0)
                if j >= 1:
                    emit_group(j - 1, 1, mid=lambda j=j: emit_conv(j, 0))
                emit_proj_part(j, 1)
                if j == 0:
                    emit_conv(j, 0)
                if j + 1 < 8:
                    load_slot(j + 1, 1)
                if j >= 1:
                    emit_group(j - 1, 2)
                emit_proj_part(j, 2)
                if j >= 1:
                    emit_group(j - 1, 3, mid=lambda j=j: (emit_conv(j, 1), emit_xcb(j)))
                emit_proj_part(j, 3)
                if j == 0:
                    emit_conv(j, 1)
                    emit_xcb(j)
                if j + 1 < 8:
                    load_gw(j + 1)
            pre7 = {0: oproj_load(wol_d, 0)}
            for g in range(4):
                emit_group(7, g)

            tr(banks[7][:, 0:128], fin[:], identf[:], [r_fin, r_const], [bres[7]], signal=True)
            r_fino = Res()
            cp("act", fino[:], banks[7][:, 0:128], [bres[7]], [r_fino])
            rr = Res()
            P.dma("sp", lambda e: e.dma_start(out=st_o, in_=fino[:]), reads=[r_fino], writes=[rr])
            finals.append(rr)

            chk(6)
            deadw = retire(xbp_res + xc_res + xcb_res + g2_res + [hf_res, hb_res] + trt_res + tit_res + at_res +
                           sqt6_res + g2b_res[1])
            m1T = v3(A1[:], 16)
            m1_res = [[Res(deps=deadw) for _ in range(2)] for _ in range(16)]
            deadh1 = retire([r for row in h1_res for r in row])
            xs7 = [A2[:, i * 2048:(i + 1) * 2048] for i in range(2)]
            xs7_res = [Res(deps=deadh1) for _ in range(2)]
            sq7 = [A2[:, 6144:7168].bitcast(BF16)[:, i * 512:(i + 1) * 512] for i in range(3)]
            sq7_res = [Res(deps=deadh1) for _ in range(3)]
            rstd7 = [A2[:, 7168 + i * 512:7168 + (i + 1) * 512] for i in range(2)]
            rstd7_res = [Res(deps=deadh1) for _ in range(2)]
            p7_acc = [A2[:, 4096 + n * 512:4096 + (n + 1) * 512] for n in range(4)]
            p7_acc_res = [Res(deps=deadh1) for _ in range(4)]
            p7_n = [0]

            def p7_hook(sidx):
                while bg_pending:
                    bg_pending.pop(0)()
                if bg_jobs and sidx < 6:
                    l_, g_ = bg_jobs.pop(0)
                    n_ = p7_n[0]
                    p7_n[0] += 1
                    for _ in adaln_group_dve(l_, g_, 7, acc=p7_acc[n_ % 4], acc_res=p7_acc_res[n_ % 4]):
                        pass
            outproj(wol_d, ogT, og_res, m1T, m1_res, sq7, sq7_res, rstd7, rstd7_res, before_slot=p7_hook,
                    pre_slots=pre7)
            while bg_jobs:
                p7_hook(0)
            while bg_pending:
                bg_pending.pop(0)()
            def ew7(c, half):
                hs = slice(half * 512, (half + 1) * 512)
                stt(m1T[:, c, hs], m1T[:, c, hs], ggv[1][:, c:c + 1], rstd7[half], ALU.mult, ALU.mult,
                    [m1_res[c][half], r_modg[1], rstd7_res[half]], [m1_res[c][half]])
                tt("dve", m1T[:, c, hs], m1T[:, c, hs], d0T[:, c, hs], ALU.add,
                   [m1_res[c][half], d0_res[c][half]], [m1_res[c][half]])

            deadog = retire([r for row in og_res for r in row])
            os7 = [A3[:, i * 2048:(i + 1) * 2048] for i in range(3)]
            os7_res = [Res(deps=deadog) for _ in range(3)]

            def load7(tb):
                si = tb % 2
                P.dma("sp", lambda e: e.dma_start(out=xs7[si], in_=x_d[tb * 128:(tb + 1) * 128, :]),
                      writes=[xs7_res[si]])

            load7(0)
            load7(1)
            for c in range(16):
                ew7(c, 0)
            for tb in range(8):
                si = tb % 2
                oi = tb % 3
                half = tb // 4
                for q in range(4):
                    for j in range(4):
                        c = 4 * q + j
                        tr(banks[q][:, j * 128:(j + 1) * 128], m1T[:, c, tb * 128:(tb + 1) * 128], identf[:],
                           [m1_res[c][half], r_const], [bres[q]], signal=(j == 3))
                    tt("dve", os7[oi][:, q * 512:(q + 1) * 512], banks[q][:], xs7[si][:, q * 512:(q + 1) * 512],
                       ALU.add, [bres[q], xs7_res[si]], [os7_res[oi]])
                if tb + 2 < 8:
                    load7(tb + 2)
                rr = Res()
                P.dma("sp", lambda e, tb=tb, oi=oi: e.dma_start(out=y_o[tb * 128:(tb + 1) * 128, :], in_=os7[oi]),
                      reads=[os7_res[oi]], writes=[rr])
                finals.append(rr)
                if tb < 4:
                    for c in range(4 * tb, 4 * tb + 4):
                        ew7(c, 1)

        except _Stop:
            pass
        P.emit(nc, final_waits=finals)
    return nc


def _pcol(v):
    v = np.asarray(v, np.float32).reshape(-1, 128)
    return np.ascontiguousarray(v.T)


def _rope_tables(n):
    grid_w = 64
    rows = n // grid_w
    row = np.repeat(np.arange(rows, dtype=np.float32), grid_w)
    col = np.tile(np.arange(grid_w, dtype=np.float32), rows)
    inv = (np.float32(10000.0) ** (-np.arange(0, 64, 2, dtype=np.float32) / np.float32(64))).astype(np.float32)
    ar = row[:, None] * inv
    ac = col[:, None] * inv
    ang = np.concatenate([ar, ar, ac, ac], axis=-1).astype(np.float32)
    return np.cos(ang).astype(np.float32), np.sin(ang).astype(np.float32)


def _prep_inputs(x_prompt, x_sample, c, cache_k, cache_v, state_lru, c_ctx, w_mod, b_mod, g_pre, g_post,
                 w_in_attn, q_norm, k_norm, w_out_attn, w_in_lru, conv_w, conv_b,
                 w_rg_a, b_rg_a, w_rg_x, b_rg_x, lru_lambda, w_out_lru):
    f = lambda a: np.ascontiguousarray(np.asarray(a, dtype=np.float32))
    ident = np.eye(128, dtype=np.float32)
    rperm = np.zeros((128, 128), np.float32)
    for dp in range(128):
        base = (dp // 64) * 64
        r = dp - base
        if r < 32:
            rperm[base + r + 32, dp] = -1.0
        else:
            rperm[base + r - 32, dp] = 1.0
    pp = np.zeros((128, NPP), np.float32)

    def put(name, arr):
        o, w = PP[name]
        assert arr.shape == (128, w), (name, arr.shape)
        pp[:, o:o + w] = arr
    put("bmod0", _pcol(b_mod[0]))
    put("bmod1", _pcol(b_mod[1]))
    put("gpre0", _pcol(g_pre[0]))
    put("gpre1", _pcol(g_pre[1]))
    put("gpost0", _pcol(g_post[0]))
    put("gpost1", _pcol(g_post[1]))
    put("qn", _pcol(q_norm[0]))
    put("kn", _pcol(k_norm[0]))
    put("convw", np.concatenate([_pcol(conv_w[0, j]) for j in range(4)], axis=1))
    put("convb", _pcol(conv_b[0]))
    put("ba", np.concatenate([_pcol(b_rg_a[0, d]) for d in range(2)], axis=1))
    put("bx", np.concatenate([_pcol(b_rg_x[0, d]) for d in range(2)], axis=1))
    put("lam", np.concatenate([_pcol(lru_lambda[0, d]) for d in range(2)], axis=1))

    cos_s, sin_s = _rope_tables(1024)
    shared = {
        "ident": ident, "rperm": rperm, "pp": pp,
        "w_mod": f(w_mod), "w_in_attn": f(w_in_attn[0]), "w_out_attn": f(w_out_attn[0]),
        "w_in_lru": f(w_in_lru[0]), "w_out_lru": f(w_out_lru[0]), "w_rg_a": f(w_rg_a[0]), "w_rg_x": f(w_rg_x[0]),
    }
    in_maps = []
    for core in range(8):
        m = dict(shared)
        if core < 4:
            m["x"] = f(x_prompt[4 * core:4 * core + 4]).reshape(NT, D)
            m["cond"] = _pcol(c_ctx)
            m["ck"] = np.zeros((512, 512), np.float32)
            m["cv"] = np.zeros((512, 512), np.float32)
            m["h0"] = np.zeros((128, 32), np.float32)
            fl = np.zeros((128, 8), np.float32)
            fl[:, 0] = 0.0
            fl[:, 1] = -1.0
            mb = np.zeros((12, 4), np.float32)
            for kb in range(4, 12):
                mb[kb, (kb - 4) // 2] = 1.0
            m["cosT"] = np.ones((128, NT), np.float32)
            m["sinT"] = np.zeros((128, NT), np.float32)
        else:
            b = core - 4
            m["x"] = f(x_sample[b])
            m["cond"] = _pcol(c[b])
            m["ck"] = f(cache_k[b, 0]).reshape(512, 512)
            m["cv"] = f(cache_v[b, 0]).reshape(512, 512)
            m["h0"] = np.concatenate([_pcol(state_lru[b, 0, d]) for d in range(2)], axis=1)
            fl = np.zeros((128, 8), np.float32)
            fl[:, 0] = 1.0
            fl[:, 1] = 0.0
            mb = np.ones((12, 4), np.float32)
            m["cosT"] = np.ascontiguousarray(cos_s.T)
            m["sinT"] = np.ascontiguousarray(sin_s.T)
        m["flags"] = fl
        m["maskb"] = np.ascontiguousarray(np.broadcast_to(mb.reshape(1, 48), (128, 48))).astype(np.float32)
        in_maps.append(m)
    return in_maps


_CACHE = {}


def kernel(**inputs):
    in_maps = _prep_inputs(**inputs)
    if "nc" not in _CACHE:
        _CACHE["nc"] = build_program(2)
    nc = _CACHE["nc"]
    res = run_bass_kernel_spmd(nc, in_maps, core_ids=list(range(8)))
    r = res.results
    y_p = np.stack([r[i]["y_out"] for i in range(4)]).reshape(16, 256, D).astype(np.float32)
    y_s = np.stack([r[4 + i]["y_out"] for i in range(4)]).reshape(4, 1024, D).astype(np.float32)
    nk = np.stack([r[i]["k_out"] for i in range(4)]).reshape(16, 1, 256, 4, 128).astype(np.float32)
    nv = np.stack([r[i]["v_out"] for i in range(4)]).reshape(16, 1, 256, 4, 128).astype(np.float32)
    ns = np.stack([r[i]["st_out"] for i in range(4)]).reshape(4, 4, 2, 16 * 128).reshape(16, 1, 2, D).astype(np.float32)
    return (y_p, y_s, nk, nv, ns)
```
